# Optimizing a Trainium2 kernel written in Bass

```python
import math
import jax
import jax.numpy as jnp
from jax import lax
import numpy as np

D_MODEL = 2048
BATCH = 4
SEQ = 4096
DEPTH = 2

GRID_W = 64
CTX_LEN = 256
HEAD_DIM = 128
NA_HEADS = 4
DN_HEADS = 4
FT_GROUPS = 4
SG_GROUPS = 4
N_BRANCH = 4
W_NA = NA_HEADS * HEAD_DIM
W_DN = DN_HEADS * HEAD_DIM
W_FT = FT_GROUPS * HEAD_DIM
W_SG = SG_GROUPS * HEAD_DIM
W_BR = W_NA
NA_WIN_R = 8
NA_WIN_C = 16
DN_CHUNK = 64
DN_CONV = 5
SG_CHUNK = 128
D_FF = ((8 * D_MODEL + 3 * 256 - 1) // (3 * 256)) * 256
N_MOD = 6
IN_SPLITS = (W_NA, W_NA, W_NA, 3 * W_DN, W_DN, 2 * DN_HEADS, 2 * DN_HEADS, W_FT, W_SG, W_SG)
D_IN = sum(IN_SPLITS)
EPS = 1e-6
NEG_INF = -1e30

kernel_name = "hybrid_dit_natten_deltanet_fnet_gmlp"


def _rmsnorm(x, w):
    xf = x.astype(jnp.float32)
    y = xf * lax.rsqrt(jnp.mean(xf * xf, axis=-1, keepdims=True) + EPS)
    return (y * w.astype(jnp.float32)).astype(x.dtype)


def _layernorm(x, w):
    xf = x.astype(jnp.float32)
    mu = jnp.mean(xf, axis=-1, keepdims=True)
    var = jnp.mean(jnp.square(xf - mu), axis=-1, keepdims=True)
    return ((xf - mu) * lax.rsqrt(var + EPS) * w.astype(jnp.float32)).astype(x.dtype)


def _l2norm(x):
    return x * lax.rsqrt(jnp.sum(x * x, axis=-1, keepdims=True) + EPS)


def _heads(x, n):
    return x.reshape(x.shape[0], x.shape[1], n, -1)


def _dwconv_centred(x, w):
    k = w.shape[0]
    return lax.conv_general_dilated(
        x, w[:, None, :].astype(x.dtype), window_strides=(1,), padding=[(k // 2, k // 2)],
        dimension_numbers=('NWC', 'WIO', 'NWC'), feature_group_count=x.shape[-1])


def _swiglu(h, w1, w3, w2):
    return (jax.nn.silu(h @ w1) * (h @ w3)) @ w2


def _neighbourhood_attention(q, k, v, k_ctx, v_ctx, rpb):
    B, S, H, dh = q.shape
    rows = S // GRID_W
    kr = min(NA_WIN_R, rows)
    scale = dh ** -0.5
    r = np.arange(rows)
    row_idx = np.clip(r - kr // 2, 0, rows - kr)[:, None] + np.arange(kr)[None, :]
    col = np.arange(GRID_W)
    col_start = np.clip(col - NA_WIN_C // 2, 0, GRID_W - NA_WIN_C)
    in_win = (col[None, :] >= col_start[:, None]) & (col[None, :] < col_start[:, None] + NA_WIN_C)
    mask = np.tile(in_win, (1, kr))
    dr = row_idx - r[:, None]
    dc = np.clip(col[None, :] - col[:, None], 1 - NA_WIN_C, NA_WIN_C - 1)
    bias = rpb[:, dr[:, None, :, None] + NA_WIN_R - 1, dc[None, :, None, :] + NA_WIN_C - 1]
    bias = bias.reshape(H, rows, GRID_W, kr * GRID_W).astype(jnp.float32)

    qg = q.reshape(B, rows, GRID_W, H, dh)
    kb = k.reshape(B, rows, GRID_W, H, dh)[:, row_idx].reshape(B, rows, kr * GRID_W, H, dh)
    vb = v.reshape(B, rows, GRID_W, H, dh)[:, row_idx].reshape(B, rows, kr * GRID_W, H, dh)
    s_loc = jnp.einsum('brqhd,brkhd->bhrqk', qg, kb, preferred_element_type=jnp.float32) * scale + bias[None]
    s_loc = jnp.where(mask, s_loc, NEG_INF)
    s_ctx = jnp.einsum('brqhd,blhd->bhrql', qg, k_ctx, preferred_element_type=jnp.float32) * scale
    p = jax.nn.softmax(jnp.concatenate([s_loc, s_ctx], axis=-1), axis=-1).astype(v.dtype)
    nk = kr * GRID_W
    o = (jnp.einsum('bhrqk,brkhd->brqhd', p[..., :nk], vb)
         + jnp.einsum('bhrql,blhd->brqhd', p[..., nk:], v_ctx))
    return o.reshape(B, S, H * dh)


def _dense_attention(q, k, v):
    B, L, H, dh = q.shape
    s = jnp.einsum('bqhd,bkhd->bhqk', q, k, preferred_element_type=jnp.float32) * dh ** -0.5
    p = jax.nn.softmax(s, axis=-1).astype(v.dtype)
    return jnp.einsum('bhqk,bkhd->bqhd', p, v).reshape(B, L, H * dh)


def _gated_delta_chunked(q, k, v, g, beta, s0):
    B, T, H, dk = q.shape
    dv = v.shape[-1]
    C = DN_CHUNK
    n = T // C

    def chunks(a):
        a = a.reshape((B, n, C, H) + a.shape[3:])
        return jnp.moveaxis(a, (1, 3), (0, 2))

    q = chunks(q) * dk ** -0.5
    k = chunks(k)
    v = chunks(v)
    beta = chunks(beta)
    gc = jnp.cumsum(chunks(g), axis=-1)
    incl = np.tril(np.ones((C, C), bool))
    strict = np.tril(np.ones((C, C), bool), -1)
    decay = jnp.exp(jnp.where(incl, gc[..., :, None] - gc[..., None, :], -jnp.inf))
    k_beta = k * beta[..., None]
    a_mat = jnp.where(strict, jnp.einsum('nbhid,nbhjd->nbhij', k_beta, k) * decay, 0.0)
    eye = jnp.eye(C, dtype=q.dtype)
    t_mat = lax.linalg.triangular_solve(eye + a_mat, jnp.broadcast_to(eye, a_mat.shape),
                                        left_side=True, lower=True, unit_diagonal=True)
    u = t_mat @ (v * beta[..., None])
    w = t_mat @ (k_beta * jnp.exp(gc)[..., None])
    qk = jnp.where(incl, jnp.einsum('nbhid,nbhjd->nbhij', q, k) * decay, 0.0)

    def step(s, inp):
        q_c, k_c, u_c, w_c, g_c, qk_c = inp
        v_new = u_c - w_c @ s
        o_c = (q_c * jnp.exp(g_c)[..., None]) @ s + qk_c @ v_new
        g_last = g_c[..., -1:]
        s = (s * jnp.exp(g_last)[..., None]
             + jnp.einsum('bhcd,bhce->bhde', k_c * jnp.exp(g_last - g_c)[..., None], v_new))
        return s, o_c

    s_fin, o = lax.scan(step, s0, (q, k, u, w, gc, qk))
    o = jnp.moveaxis(o, (0, 2), (1, 3)).reshape(B, T, H, dv)
    return o, s_fin


def _bidir_deltanet(qkv, a, b, conv_w, a_log, dt_bias, s0_fwd, s0_bwd):
    f32 = jnp.float32
    qkv = jax.nn.silu(_dwconv_centred(qkv, conv_w)).astype(f32)
    q, k, v = jnp.split(qkv, 3, axis=-1)
    q = _l2norm(_heads(q, DN_HEADS))
    k = _l2norm(_heads(k, DN_HEADS))
    v = _heads(v, DN_HEADS)
    g = -jnp.exp(a_log.astype(f32)) * jax.nn.softplus(_heads(a.astype(f32), 2) + dt_bias.astype(f32))
    beta = jax.nn.sigmoid(_heads(b.astype(f32), 2))
    o_f, s_f = _gated_delta_chunked(q, k, v, g[:, :, 0], beta[:, :, 0], s0_fwd)
    o_b, s_b = _gated_delta_chunked(q[:, ::-1], k[:, ::-1], v[:, ::-1], g[:, ::-1, 1], beta[:, ::-1, 1], s0_bwd)
    return o_f + o_b[:, ::-1], s_f, s_b


def _gated_rmsnorm(o, z, w):
    B, T = z.shape[0], z.shape[1]
    y = _rmsnorm(o, w) * jax.nn.silu(_heads(z, DN_HEADS).astype(jnp.float32))
    return y.reshape(B, T, W_DN).astype(z.dtype)


def _fourier(f):
    B, T, _ = f.shape
    ff = f.astype(jnp.float32).reshape(B, T, FT_GROUPS, HEAD_DIM)
    y = jnp.fft.fftn(ff, axes=(1, 3), norm='ortho').real
    return y.reshape(B, T, W_FT).astype(f.dtype)


def _spatial_gating(u, v, w_s, b_s, v_norm):
    B, T, _ = u.shape
    n = T // SG_CHUNK
    u = jax.nn.gelu(u)
    v = _layernorm(jax.nn.gelu(v), v_norm)
    vc = v.reshape(B, n, SG_CHUNK, SG_GROUPS, HEAD_DIM)
    mix = jnp.einsum('gts,bnsgd->bntgd', w_s, vc) + b_s.T[:, :, None]
    return u * mix.reshape(B, T, W_SG)


def _hybrid_mixer(h, hc, w_in, na_qnorm, na_knorm, na_rpb, dn_conv, dn_a_log, dn_dt_bias, dn_onorm,
                  sg_w, sg_b, sg_vnorm, w_gate, b_gate, w_branch, w_out, need_ctx):
    offsets = [int(o) for o in np.cumsum(IN_SPLITS)[:-1]]
    na_q, na_k, na_v, dn_qkv, dn_z, dn_a, dn_b, ft, sg_u, sg_v = jnp.split(h @ w_in, offsets, axis=-1)
    cna_q, cna_k, cna_v, cdn_qkv, cdn_z, cdn_a, cdn_b, cft, csg_u, csg_v = jnp.split(hc @ w_in, offsets, axis=-1)

    k_c = _rmsnorm(_heads(cna_k, NA_HEADS), na_knorm)
    v_c = _heads(cna_v, NA_HEADS)
    y_na = _neighbourhood_attention(_rmsnorm(_heads(na_q, NA_HEADS), na_qnorm),
                                    _rmsnorm(_heads(na_k, NA_HEADS), na_knorm),
                                    _heads(na_v, NA_HEADS), k_c, v_c, na_rpb)

    zero = jnp.zeros((hc.shape[0], DN_HEADS, HEAD_DIM, HEAD_DIM), jnp.float32)
    o_dn_c, s_f, s_b = _bidir_deltanet(cdn_qkv, cdn_a, cdn_b, dn_conv, dn_a_log, dn_dt_bias, zero, zero)
    o_dn, _, _ = _bidir_deltanet(dn_qkv, dn_a, dn_b, dn_conv, dn_a_log, dn_dt_bias, s_f, s_b)
    y_dn = _gated_rmsnorm(o_dn, dn_z, dn_onorm)

    def merge(hh, branches):
        merged = None
        for i, y in enumerate(branches):
            term = jax.nn.sigmoid(hh @ w_gate[i] + b_gate[i]) * (y @ w_branch[i])
            merged = term if merged is None else merged + term
        return merged @ w_out

    y = merge(h, (y_na, y_dn, _fourier(ft), _spatial_gating(sg_u, sg_v, sg_w, sg_b, sg_vnorm)))
    if not need_ctx:
        return y, None
    y_na_c = _dense_attention(_rmsnorm(_heads(cna_q, NA_HEADS), na_qnorm), k_c, v_c)
    yc = merge(hc, (y_na_c, _gated_rmsnorm(o_dn_c, cdn_z, dn_onorm), _fourier(cft),
                    _spatial_gating(csg_u, csg_v, sg_w, sg_b, sg_vnorm)))
    return y, yc


def setup_inputs(seed: int = 0) -> dict:
    key = jax.random.key(seed)
    ks = iter(jax.random.split(key, 40))
    f32 = jnp.float32

    def nrm(shape, scale):
        return jax.random.normal(next(ks), shape, f32) * scale

    def gain(shape):
        return 1.0 + nrm(shape, 0.02)

    dt = jnp.exp(jax.random.uniform(next(ks), (DEPTH, 2, DN_HEADS), f32, math.log(1e-3), math.log(1e-1)))
    return {
        "x": nrm((BATCH, SEQ, D_MODEL), 1.0),
        "c": nrm((BATCH, D_MODEL), 1.0),
        "ctx": nrm((BATCH, CTX_LEN, D_MODEL), 1.0),
        "c_ctx": nrm((D_MODEL,), 1.0),
        "w_ada": nrm((DEPTH, D_MODEL, N_MOD * D_MODEL), 0.5 * D_MODEL ** -0.5),
        "b_ada": nrm((DEPTH, N_MOD * D_MODEL), 0.02),
        "norm1_w": gain((DEPTH, D_MODEL)),
        "norm2_w": gain((DEPTH, D_MODEL)),
        "w_in": nrm((DEPTH, D_MODEL, D_IN), D_MODEL ** -0.5),
        "na_qnorm": gain((DEPTH, HEAD_DIM)),
        "na_knorm": gain((DEPTH, HEAD_DIM)),
        "na_rpb": nrm((DEPTH, NA_HEADS, 2 * NA_WIN_R - 1, 2 * NA_WIN_C - 1), 0.5),
        "dn_conv": nrm((DEPTH, DN_CONV, 3 * W_DN), DN_CONV ** -0.5),
        "dn_a_log": jnp.log(jax.random.uniform(next(ks), (DEPTH, 2, DN_HEADS), f32, 1.0, 16.0)),
        "dn_dt_bias": dt + jnp.log(-jnp.expm1(-dt)),
        "dn_onorm": gain((DEPTH, HEAD_DIM)),
        "sg_w": nrm((DEPTH, SG_GROUPS, SG_CHUNK, SG_CHUNK), SG_CHUNK ** -0.5),
        "sg_b": nrm((DEPTH, SG_GROUPS, SG_CHUNK), 0.02),
        "sg_vnorm": gain((DEPTH, W_SG)),
        "w_gate": nrm((DEPTH, N_BRANCH, D_MODEL, D_MODEL), D_MODEL ** -0.5),
        "b_gate": nrm((DEPTH, N_BRANCH, D_MODEL), 0.02),
        "w_branch": nrm((DEPTH, N_BRANCH, W_BR, D_MODEL), W_BR ** -0.5),
        "w_out": nrm((DEPTH, D_MODEL, D_MODEL), D_MODEL ** -0.5),
        "ffn_w1": nrm((DEPTH, D_MODEL, D_FF), D_MODEL ** -0.5),
        "ffn_w3": nrm((DEPTH, D_MODEL, D_FF), D_MODEL ** -0.5),
        "ffn_w2": nrm((DEPTH, D_FF, D_MODEL), D_FF ** -0.5),
    }


def reference(x, c, ctx, c_ctx, w_ada, b_ada, norm1_w, norm2_w, w_in, na_qnorm, na_knorm, na_rpb,
              dn_conv, dn_a_log, dn_dt_bias, dn_onorm, sg_w, sg_b, sg_vnorm, w_gate, b_gate, w_branch,
              w_out, ffn_w1, ffn_w3, ffn_w2):
    for l in range(DEPTH):
        need_ctx = l < DEPTH - 1
        mod_l = (jax.nn.silu(c) @ w_ada[l] + b_ada[l])[:, None, :]
        mod_c = (jax.nn.silu(c_ctx) @ w_ada[l] + b_ada[l])[None, None, :]
        sh1, sc1, g1, sh2, sc2, g2 = jnp.split(mod_l, N_MOD, axis=-1)
        csh1, csc1, cg1, csh2, csc2, cg2 = jnp.split(mod_c, N_MOD, axis=-1)

        h = _rmsnorm(x, norm1_w[l]) * (1.0 + sc1) + sh1
        hc = _rmsnorm(ctx, norm1_w[l]) * (1.0 + csc1) + csh1
        y, yc = _hybrid_mixer(h, hc, w_in[l], na_qnorm[l], na_knorm[l], na_rpb[l], dn_conv[l], dn_a_log[l],
                              dn_dt_bias[l], dn_onorm[l], sg_w[l], sg_b[l], sg_vnorm[l], w_gate[l], b_gate[l],
                              w_branch[l], w_out[l], need_ctx)
        x = x + g1 * y
        h = _rmsnorm(x, norm2_w[l]) * (1.0 + sc2) + sh2
        x = x + g2 * _swiglu(h, ffn_w1[l], ffn_w3[l], ffn_w2[l])
        if need_ctx:
            ctx = ctx + cg1 * yc
            hc = _rmsnorm(ctx, norm2_w[l]) * (1.0 + csc2) + csh2
            ctx = ctx + cg2 * _swiglu(hc, ffn_w1[l], ffn_w3[l], ffn_w2[l])
    return x
```

```python
import numpy as np
import ml_dtypes
from contextlib import ExitStack
import concourse.bass as bass
import concourse.mybir as mybir
from concourse.bass_utils import run_bass_kernel_spmd

F32 = mybir.dt.float32
BF16 = mybir.dt.bfloat16
I32 = mybir.dt.int32
AF = mybir.ActivationFunctionType
ALU = mybir.AluOpType
AX = mybir.AxisListType
NCORES = 8
PAIRS = [[0, 1], [2, 3], [4, 5], [6, 7]]
ALL8 = [list(range(8))]
PAIRS_B = [[0, 4], [1, 5], [2, 6], [3, 7]]


class Buf:
    __slots__ = ("t", "w", "r", "name")

    def __init__(self, t, name):
        self.t = t
        self.name = name
        self.w = {}
        self.r = {}

    def __getitem__(self, idx):
        return V(self, self.t[idx])


class V:
    __slots__ = ("buf", "ap")

    def __init__(self, buf, ap):
        self.buf = buf
        self.ap = ap

    def __getitem__(self, idx):
        return V(self.buf, self.ap[idx])

    def rearrange(self, *a, **k):
        return V(self.buf, self.ap.rearrange(*a, **k))

    def to_broadcast(self, shape):
        return V(self.buf, self.ap.to_broadcast(shape))


def _ap(x):
    return x.ap if isinstance(x, V) else x


class Eng:
    def __init__(self, name, eng, sem, sid):
        self.name = name
        self.eng = eng
        self.sem = sem
        self.sid = sid
        self.count = 0
        self.known = {}


class FW:
    def __init__(self, nc, n_dma_ring=12):
        self.nc = nc
        self.es = ExitStack()
        self.scopes = []
        self.sems = {}
        self.engs = {}
        self._nsem = 0
        for name, eng in (("pe", nc.tensor), ("act", nc.scalar), ("dve", nc.vector),
                          ("pool", nc.gpsimd), ("sp", nc.sync)):
            sid, sem = self._new_sem("e_" + name)
            self.engs[name] = Eng(name, eng, sem, sid)
        self.rings = {}
        for q in ("sp", "pool", "act"):
            ring = []
            for i in range(n_dma_ring):
                sid, sem = self._new_sem("d_%s%d" % (q, i))
                ring.append([sid, sem, 0])
            self.rings[q] = [ring, 0]
        self.bar_sid, self.bar_sem = self._new_sem("bar")
        self.bar_n = 0
        self.cc_sid, self.cc_sem = self._new_sem("cc")
        self.cc_n = 0
        self.psum_bufs = []
        self.psum_i = 0
        self.n_ins = 0
        self._uid = 0

    def _new_sem(self, name):
        sem = self.es.enter_context(self.nc.semaphore(name))
        sid = self._nsem
        self._nsem += 1
        self.sems[sid] = sem
        return sid, sem

    def push_scope(self):
        self.scopes.append(ExitStack())

    def pop_scope(self):
        self.barrier()
        self.scopes.pop().close()

    def sbuf(self, name, shape, dtype):
        st = self.scopes[-1] if self.scopes else self.es
        self._uid += 1
        t = st.enter_context(self.nc.sbuf_tensor("%s_%d" % (name, self._uid), list(shape), dtype))
        return Buf(t, name)

    def init_psum(self, n=8):
        for i in range(n):
            t = self.es.enter_context(self.nc.psum_tensor("ps%d" % i, [128, 512], F32))
            self.psum_bufs.append(Buf(t, "ps%d" % i))

    def psum(self):
        b = self.psum_bufs[self.psum_i % len(self.psum_bufs)]
        self.psum_i += 1
        return b

    def _wait(self, e, deps):
        for sid, val in deps.items():
            if e.name == "pe" and sid == e.sid:
                continue
            if e.known.get(sid, 0) < val:
                e.eng.wait_ge(self.sems[sid], val)
                e.known[sid] = val

    @staticmethod
    def _merge(dst, src):
        for k, v in src.items():
            if dst.get(k, 0) < v:
                dst[k] = v

    def _collect(self, reads, writes):
        deps = {}
        for v in reads:
            if isinstance(v, V):
                self._merge(deps, v.buf.w)
        for v, append in writes:
            if isinstance(v, V):
                self._merge(deps, v.buf.r)
                if not append:
                    self._merge(deps, v.buf.w)
        return deps

    def _commit(self, sid, val, reads, writes):
        for v in reads:
            if isinstance(v, V):
                b = v.buf
                if b.r.get(sid, 0) < val:
                    b.r[sid] = val
        for v, append in writes:
            if isinstance(v, V):
                b = v.buf
                if not append:
                    b.w = {sid: val}
                    b.r = {}
                elif b.w.get(sid, 0) < val:
                    b.w[sid] = val

    def op(self, eng, fn, reads=(), writes=(), append=False, signal=True):
        e = self.engs[eng]
        wr = [(w, append) for w in writes]
        self._wait(e, self._collect(reads, wr))
        ins = fn(e.eng)
        self.n_ins += 1
        if signal:
            e.count += 1
            ins.then_inc(e.sem, 1)
            val = e.count
        else:
            val = e.count + 1
        self._commit(e.sid, val, reads, wr)
        return ins

    def dma(self, out, in_, queue="sp", append=False, **kw):
        e = self.engs[queue]
        ring, idx = self.rings[queue]
        slot = ring[idx % len(ring)]
        self.rings[queue][1] = idx + 1
        reads = [in_]
        wr = [(out, append)]
        deps = self._collect(reads, wr)
        if slot[2] > 0:
            self._merge(deps, {slot[0]: slot[2]})
        self._wait(e, deps)
        ins = e.eng.dma_start(out=_ap(out), in_=_ap(in_), **kw)
        slot[2] += 16
        ins.then_inc(slot[1], 16)
        self.n_ins += 1
        self._commit(slot[0], slot[2], reads, wr)
        return ins

    def barrier(self):
        sp = self.engs["sp"]
        deps = {}
        for e in self.engs.values():
            if e.count > 0:
                deps[e.sid] = e.count
        for q, (ring, _) in self.rings.items():
            for slot in ring:
                if slot[2] > 0:
                    deps[slot[0]] = slot[2]
        if self.cc_n > 0:
            deps[self.cc_sid] = self.cc_n
        self._wait(sp, deps)
        self.bar_n += 1
        sp.eng.sem_inc(self.bar_sem, 1)
        for e in self.engs.values():
            if e is not sp:
                e.eng.wait_ge(self.bar_sem, self.bar_n)
            for sid, val in deps.items():
                e.known[sid] = max(e.known.get(sid, 0), val)

    def cc(self, kind, groups, src, dst):
        pool = self.engs["pool"]
        ins = pool.eng.collective_compute(kind, ALU.bypass, replica_groups=groups, ins=[src], outs=[dst])
        self.cc_n += 1
        ins.then_inc(self.cc_sem, 1)
        pool.eng.wait_ge(self.cc_sem, self.cc_n)
        pool.known[self.cc_sid] = self.cc_n

    def collective(self, kind, groups, src, dst):
        self.barrier()
        self.cc(kind, groups, src, dst)
        self.barrier()

    def act(self, out, in_, func, bias=0.0, scale=1.0, append=False):
        return self.op("act", lambda E: E.activation(out=_ap(out), in_=_ap(in_), func=func, bias=_ap(bias),
                                                     scale=_ap(scale)),
                       reads=[in_, bias, scale], writes=[out], append=append)

    def tt(self, out, in0, in1, op, eng="dve", append=False):
        return self.op(eng, lambda E: E.tensor_tensor(out=_ap(out), in0=_ap(in0), in1=_ap(in1), op=op),
                       reads=[in0, in1], writes=[out], append=append)

    def ts(self, out, in0, s1, s2=None, op0=ALU.mult, op1=None, eng="dve", append=False):
        kw = {}
        if op1 is not None:
            kw["op1"] = op1
        return self.op(eng, lambda E: E.tensor_scalar(out=_ap(out), in0=_ap(in0), scalar1=_ap(s1),
                                                      scalar2=_ap(s2), op0=op0, **kw),
                       reads=[in0, s1, s2], writes=[out], append=append)

    def stt(self, out, in0, scalar, in1, op0, op1, eng="dve", append=False):
        eng = "dve"
        return self.op(eng, lambda E: E.scalar_tensor_tensor(out=_ap(out), in0=_ap(in0), scalar=_ap(scalar),
                                                             in1=_ap(in1), op0=op0, op1=op1),
                       reads=[in0, scalar, in1], writes=[out], append=append)

    def copy(self, out, in_, eng="dve", append=False):
        if eng == "act":
            return self.act(out, in_, AF.Copy, append=append)
        return self.op(eng, lambda E: E.tensor_copy(out=_ap(out), in_=_ap(in_)),
                       reads=[in_], writes=[out], append=append)

    def recip(self, out, in_, append=False):
        return self.op("dve", lambda E: E.reciprocal(out=_ap(out), in_=_ap(in_)),
                       reads=[in_], writes=[out], append=append)

    def memset(self, out, val, eng="pool", append=False):
        return self.op(eng, lambda E: E.memset(_ap(out), val), reads=[], writes=[out], append=append)

    def mm(self, out, lhsT, rhs, start=True, stop=True, sig=True):
        return self.op("pe", lambda E: E.matmul(_ap(out), lhsT=_ap(lhsT), rhs=_ap(rhs), start=start, stop=stop),
                       reads=[lhsT, rhs], writes=[out], append=not start, signal=(sig or stop))


class Cfg:
    def __init__(self, D=2048, ROWS=64, NL=2, L=256):
        self.D = D
        self.KT = D // 128
        self.ROWS = ROWS
        self.T = ROWS * 64
        self.Th = self.T
        self.L = L
        self.TL = self.Th + L
        self.NL = NL
        self.DFF = ((8 * D + 3 * 256 - 1) // (3 * 256)) * 256
        self.FT = self.DFF // 128
        self.DIN = 5136
        self.ncores = 8
        self.cores_per_batch = 2
        self.NTh = self.Th // 128
        self.NT = self.T // 128
        self.blocks = []
        t = 0
        while t < self.Th:
            n = min(512, self.Th - t)
            self.blocks.append((t, n, False))
            t += n
        self.blocks.append((self.Th, L, True))


def win_perm():
    o = {"na_q": 0, "na_k": 512, "na_v": 1024, "dn_q": 1536, "dn_k": 2048, "dn_v": 2560, "dn_z": 3072,
         "dn_a": 3584, "dn_b": 3592, "ft": 3600, "sg_u": 4112, "sg_v": 4624}
    cols = []
    cols += list(range(o["na_q"], o["na_q"] + 512))
    cols += list(range(o["na_k"], o["na_k"] + 512))
    for pr in range(2):
        for typ in ("dn_q", "dn_k", "dn_v", "dn_z", "ft"):
            for hh in range(2):
                h = 2 * pr + hh
                cols += list(range(o[typ] + h * 128, o[typ] + (h + 1) * 128))
    cols += list(range(o["sg_u"], o["sg_u"] + 512))
    nfm = len(cols)
    cols += list(range(o["na_v"], o["na_v"] + 512))
    cols += list(range(o["sg_v"], o["sg_v"] + 512))
    cols += list(range(o["dn_a"], o["dn_a"] + 16))
    assert len(cols) == 5136 and nfm == 4096
    return np.array(cols)


PAIR_ROWS = 1280


def dram_ap(t, offset, pat):
    return bass.AP(t, offset, [list(x) for x in pat])


class Prog:
    def __init__(self, cfg):
        self.cfg = cfg
        c = cfg
        self.nc = nc = bass.Bass("TRN2", target_bir_lowering=False)
        self.fw = fw = FW(nc)
        fw.init_psum()
        self.inp = {}
        self.dr = {}
        D, KT, NL, Th, L, TL, T = c.D, c.KT, c.NL, c.Th, c.L, c.TL, c.T
        NA8 = 6 * D
        self.NA8 = NA8
        self.NPAT = 35
        I = self._in
        I("xh", [D, Th]); I("ctxT", [D, L])
        I("scT", [128, KT * 2])
        I("w_ada_s", [NL * D, NA8]); I("b_ada_s", [128, NL * (NA8 // 128)])
        I("normw", [128, NL * 2 * KT]); I("b_gate", [128, NL * 4 * KT])
        I("w_in_s", [NL * D, c.DIN]); I("w_gate_s", [NL * 4 * D, D])
        I("w_branch_s", [NL * 4 * 512, D]); I("w_out_s", [NL * D, D])
        I("w1_s", [NL * D, c.DFF]); I("w3_s", [NL * D, c.DFF]); I("w2_s", [NL * c.DFF, D])
        I("hnorm", [128, NL * 3])
        I("na_bias", [NL * 4 * self.NPAT * 128, 128])
        I("dn_conv", [128, NL * 3 * 4 * 5])
        I("dn_par", [1, NL * 16])
        I("sg_wT", [128, NL * 4 * 128]); I("sg_b", [1, NL * 512]); I("sg_vn", [1, NL * 512])
        I("cdsd", [128, 256])
        I("ct_s", [T, T], BF16); I("st_s", [T, T], BF16)
        I("ctx_cs", [L, 2 * L], BF16)
        I("consts", [128, 9 * 128])
        self.split = getattr(cfg, "cores_per_batch", 1) == 2
        self.Th2 = Th // 2 if self.split else Th
        I("offs", [1, 4], I32)
        self.out = nc.dram_tensor("out", [D, self.Th2], F32, kind="ExternalOutput")

    def _in(self, name, shape, dt=F32):
        self.inp[name] = self.nc.dram_tensor(name, list(shape), dt, kind="ExternalInput")

    def dram(self, name, shape, dt=F32):
        t = self.nc.dram_tensor(name, list(shape), dt)
        self.dr[name] = t
        return t


def cast_and_gather(P, src_t, row0, nr, M, dst_name, chunk=4096):
    fw = P.fw
    KTt = nr // 128
    nb = M // 512
    rem = M - nb * 512
    full = P.dram(dst_name, [nb * 128, KTt * 512], BF16)
    P.Wmeta[dst_name] = (KTt, nb, rem)
    i = P.cast_i
    kstep = max(1, 4096 // 512)
    for mb in range(nb):
        for k0 in range(0, KTt, kstep):
            nk = min(kstep, KTt - k0)
            a = P.cast_in[i % len(P.cast_in)]
            b = P.cast_out[i % len(P.cast_out)]
            src = dram_ap(src_t, (row0 + k0 * 128) * M + mb * 512, [[M, 128], [128 * M, nk], [1, 512]])
            fw.dma(a[:, :nk * 512].rearrange("p (k m) -> p k m", k=nk), src, queue="sp")
            fw.copy(b[:, :nk * 512], a[:, :nk * 512], eng="dve")
            fw.dma(dram_ap(full, (mb * 128) * KTt * 512 + k0 * 512, [[KTt * 512, 128], [1, nk * 512]]), b[:, :nk * 512],
                   queue="pool")
            i += 1
    if rem:
        remt = P.dram(dst_name + "_rem", [128, KTt * rem], BF16)
        P.W[dst_name + "_rem"] = remt
        a = P.cast_in[i % len(P.cast_in)]
        b = P.cast_out[i % len(P.cast_out)]
        src = dram_ap(src_t, row0 * M + nb * 512, [[M, 128], [128 * M, KTt], [1, rem]])
        fw.dma(a[:, :KTt * rem].rearrange("p (k m) -> p k m", k=KTt), src)
        fw.copy(b[:, :KTt * rem], a[:, :KTt * rem])
        fw.dma(remt.ap()[:, :], b[:, :KTt * rem], queue="pool")
        i += 1
    P.cast_i = i
    return full, None, full


def prep_weights(P):
    c, fw = P.cfg, P.fw
    D, NL = c.D, c.NL
    P.cast_in = [fw.sbuf("cin", [128, 4096], F32) for _ in range(3)]
    P.cast_out = [fw.sbuf("cout", [128, 4096], BF16) for _ in range(3)]
    P.W = {}
    P.Wmeta = {}
    P.cast_i = 0
    todo = []
    r8 = D
    for l in range(NL):
        todo.append(("w_in%d" % l, P.inp["w_in_s"], l * r8, r8, c.DIN))
        for i in range(4):
            todo.append(("w_gate%d_%d" % (l, i), P.inp["w_gate_s"], (l * 4 + i) * r8, r8, D))
            todo.append(("w_br%d_%d" % (l, i), P.inp["w_branch_s"], (l * 4 + i) * 512, 512, D))
        todo.append(("w_out%d" % l, P.inp["w_out_s"], l * r8, r8, D))
        todo.append(("w1_%d" % l, P.inp["w1_s"], l * r8, r8, c.DFF))
        todo.append(("w3_%d" % l, P.inp["w3_s"], l * r8, r8, c.DFF))
        todo.append(("w2_%d" % l, P.inp["w2_s"], l * c.DFF, c.DFF, D))
    pairs = []
    for name, src, row0, nr, M in todo:
        send, mid, full = cast_and_gather(P, src, row0, nr, M, name)
        P.W[name] = full
        pairs.append((send, mid, full))
    P.W["ct"] = P.inp["ct_s"]
    P.W["st"] = P.inp["st_s"]


def load_consts(P):
    c, fw = P.cfg, P.fw
    NL, KT = c.NL, c.KT
    P.cst = fw.sbuf("consts", [128, 9 * 128], F32)
    fw.dma(P.cst[:, :], P.inp["consts"].ap()[:, :])
    k = lambda i: P.cst[:, i * 128:(i + 1) * 128]
    P.ident, P.Mf, P.Mb, P.Mch = k(0), k(1), k(2), k(3)
    P.ms = [k(4), k(5)]
    P.miT = [k(6), k(7)]
    P.ind2 = P.cst[:, 8 * 128:8 * 128 + 2]
    P.ones_f = fw.sbuf("ones_f", [128, 128], F32)
    fw.memset(P.ones_f[:, :], 1.0)
    P.ones_b = fw.sbuf("ones_b", [128, 128], BF16)
    fw.memset(P.ones_b[:, :], 1.0)
    P.ident_b = fw.sbuf("ident_b", [128, 128], BF16)
    fw.copy(P.ident_b[:, :], P.ident)
    P.normw = fw.sbuf("normw", [128, NL * 2 * KT], F32)
    fw.dma(P.normw[:, :], P.inp["normw"].ap()[:, :])
    P.bgate = fw.sbuf("bgate", [128, NL * 4 * KT], F32)
    fw.dma(P.bgate[:, :], P.inp["b_gate"].ap()[:, :])
    P.hnorm = fw.sbuf("hnorm", [128, NL * 3], F32)
    fw.dma(P.hnorm[:, :], P.inp["hnorm"].ap()[:, :])
    P.offs_sb = fw.sbuf("offs", [1, 4], I32)
    fw.dma(P.offs_sb[:, :], P.inp["offs"].ap()[:, :], queue="pool")
    fw.barrier()
    P.reg_half = fw.es.enter_context(P.nc.gpsimd.register("dynhalf"))
    P.nc.gpsimd.reg_load(P.reg_half, P.offs_sb.t[:1, 0:1])
    fw.barrier()


def compute_mod(P):
    c, fw, nc = P.cfg, P.fw, P.nc
    D, KT, NL = c.D, c.KT, c.NL
    NA8 = P.NA8
    nft = NA8 // 128
    NJ = 6 * KT
    P.mod = fw.sbuf("modmy", [128, NL * NJ * 2], F32)
    P.modp = fw.sbuf("modp", [128, NL * 2 * 6 * KT], F32)
    fw.push_scope()
    sc = fw.sbuf("sc", [128, KT * 2], F32)
    fw.dma(sc[:, :], P.inp["scT"].ap()[:, :])
    scs = fw.sbuf("scs", [128, KT * 2], F32)
    fw.act(scs[:, :], sc[:, :], AF.Silu)
    bsh = fw.sbuf("bsh", [128, NL * nft], F32)
    fw.dma(bsh[:, :], P.inp["b_ada_s"].ap()[:, :])
    msh = fw.sbuf("msh", [128, NL * nft * 8], F32)
    fw.memset(msh[:, :], 0.0)
    was = [fw.sbuf("wa", [128, KT * 128], F32) for _ in range(2)]
    i = 0
    for l in range(NL):
        for ft in range(nft):
            wa = was[i % 2]
            i += 1
            src = dram_ap(P.inp["w_ada_s"], l * D * NA8 + ft * 128, [[NA8, 128], [128 * NA8, KT], [1, 128]])
            fw.dma(wa[:, :].rearrange("p (k m) -> p k m", k=KT), src, queue="act")
            ps = fw.psum()
            for kt in range(KT):
                fw.mm(ps[:, 0:2], lhsT=wa[:, kt * 128:(kt + 1) * 128], rhs=scs[:, kt * 2:(kt + 1) * 2],
                      start=(kt == 0), stop=(kt == KT - 1))
            j = l * nft + ft
            fw.act(msh[:, j * 8:j * 8 + 2], ps[:, 0:2], AF.Identity, bias=bsh[:, j:j + 1], append=True)
    import os as _os
    msend = P.dram("mod_snd", [NL * nft * 128, 8])
    mfull = P.dram("mod_full", [2 * NL * nft * 128, 8])
    fw.dma(dram_ap(msend, 0, [[8, 128], [128 * 8, NL * nft], [1, 8]]),
           msh[:, :].rearrange("p (j c) -> p j c", c=8))
    prep_weights(P)
    fw.barrier()
    mfull = msend
    NJ = 6 * KT
    mall = fw.sbuf("mall", [128, NL * NJ * 8], F32)
    for l in range(NL):
        for r in range(1):
            src = dram_ap(mfull, ((r * NL + l) * nft) * 128 * 8, [[8, 128], [128 * 8, nft], [1, 8]])
            dst = mall[:, (l * NJ + r * nft) * 8:(l * NJ + (r + 1) * nft) * 8].rearrange("p (j c) -> p j c", c=8)
            fw.dma(dst, src)
    mv = mall[:, :].rearrange("p (j c) -> p j c", c=8)
    mm_ = P.mod[:, :].rearrange("p (j r) -> p j r", r=2)
    fw.copy(mm_[:, :, :], mv[:, :, 0:2])
    for l in range(NL):
        for r in range(2):
            def chunk(k):
                return mm_[:, l * NJ + k * KT:l * NJ + (k + 1) * KT, r]

            def dst(k):
                o = ((l * 2 + r) * 6 + k) * KT
                return P.modp[:, o:o + KT]
            nw1 = P.normw[:, (l * 2 + 0) * KT:(l * 2 + 1) * KT]
            nw2 = P.normw[:, (l * 2 + 1) * KT:(l * 2 + 2) * KT]
            fw.stt(dst(0), chunk(1), 1.0, nw1, op0=ALU.add, op1=ALU.mult)
            fw.copy(dst(1), chunk(0))
            fw.copy(dst(2), chunk(2))
            fw.stt(dst(3), chunk(4), 1.0, nw2, op0=ALU.add, op1=ALU.mult)
            fw.copy(dst(4), chunk(3))
            fw.copy(dst(5), chunk(5))
    fw.pop_scope()


def FWKEEP(P, name, shape, dt=F32):
    fw = P.fw
    saved = fw.scopes
    fw.scopes = []
    b = fw.sbuf(name, shape, dt)
    fw.scopes = saved
    return b


def modp(P, l, r, k, kt):
    c = P.cfg
    o = ((l * 2 + r) * 6 + k) * c.KT + kt
    return P.modp[:, o:o + 1]


def emit_norm(P, xb, n, l, r, which, hT, pools):
    c, fw = P.cfg, P.fw
    KT = c.KT
    ps = fw.psum()
    for kt in range(KT):
        sq = pools["sq"][kt % 2]
        fw.act(sq[:, :n], xb[:, kt, :n], AF.Square)
        fw.mm(ps[:, :n], lhsT=P.ones_b[:, :], rhs=sq[:, :n], start=(kt == 0), stop=(kt == KT - 1))
    rt = pools["rt"]
    fw.act(rt[:, :n], ps[:, :n], AF.Sqrt, bias=1e-6, scale=1.0 / c.D)
    rs = pools["rs"]
    fw.recip(rs[:, :n], rt[:, :n])
    ks, kb = (0, 1) if which == 0 else (3, 4)
    for kt in range(KT):
        tmp = pools["tmp"][kt % 2]
        fw.stt(tmp[:, :n], xb[:, kt, :n], modp(P, l, r, ks, kt), rs[:, :n], op0=ALU.mult, op1=ALU.mult)
        fw.act(hT[:, kt, :n], tmp[:, :n], AF.Identity, bias=modp(P, l, r, kb, kt), append=(kt > 0))


def wview(P, name, k0, nk, c0, nc_):
    KTt, nb, rem = P.Wmeta[name]
    mb = c0 // 512
    if mb < nb:
        assert nc_ == 512
        return dram_ap(P.W[name], (mb * 128) * KTt * 512 + k0 * 512, [[KTt * 512, 128], [512, nk], [1, 512]])
    assert nc_ == rem
    return dram_ap(P.W[name + "_rem"], k0 * rem, [[KTt * rem, 128], [rem, nk], [1, rem]])


def fm_rows(t, row0, ncols_total, col0, n, nrows=128):
    return dram_ap(t, row0 * ncols_total + col0, [[ncols_total, nrows], [1, n]])


def phase_a(P, l):
    c, fw = P.cfg, P.fw
    D, KT, Th, L, TL = c.D, c.KT, c.Th, c.L, c.TL
    fw.push_scope()
    xbs = [fw.sbuf("xb", [128, KT, 512], F32) for _ in range(1)]
    hTs = [fw.sbuf("hT", [128, KT, 512], BF16) for _ in range(2)]
    pools = {"sq": [fw.sbuf("sq", [128, 512], BF16) for _ in range(2)],
             "rt": fw.sbuf("rt", [128, 512], F32), "rs": fw.sbuf("rs", [128, 512], F32),
             "tmp": [fw.sbuf("tmp", [128, 512], F32) for _ in range(2)]}
    wbs = [fw.sbuf("wb", [128, KT, 512], BF16) for _ in range(3)]
    wab = fw.sbuf("wab", [128, KT, 16], BF16)
    stg = [fw.sbuf("stg", [128, 512], F32) for _ in range(4)]
    stgb = [fw.sbuf("stgb", [128, 512], BF16) for _ in range(2)]
    wname = "w_in%d" % l
    fw.dma(wab[:, :, :], wview(P, wname, 0, KT, 5120, 16))
    wi = 0
    si = 0
    for bi, (t0, n, is_ctx) in enumerate(c.blocks):
        r = 1 if is_ctx else 0
        xb = xbs[bi % len(xbs)]
        hT = hTs[bi % 2]
        if l == 0:
            src = (dram_ap(P.inp["ctxT"], 0, [[L, 128], [128 * L, KT], [1, n]]) if is_ctx else
                   dram_ap(P.inp["xh"], t0, [[Th, 128], [128 * Th, KT], [1, n]]))
        else:
            src = dram_ap(P.dr["xmy"], t0, [[TL, 128], [128 * TL, KT], [1, n]])
        fw.dma(xb[:, :, :n], src, queue="act")
        import os as _os
        suba = int(_os.environ.get("SUBA", "9"))
        emit_norm(P, xb, n, l, r, 0, hT, pools)
        if suba == 0:
            continue
        fw.dma(dram_ap(P.dr["hT"], t0, [[TL, 128], [128 * TL, KT], [1, n]]), hT[:, :, :n], queue="pool")
        if suba == 1:
            continue
        for blk in range(8):
            wb = wbs[wi % 3]
            wi += 1
            fw.dma(wb[:, :, :], wview(P, wname, 0, KT, blk * 512, 512))
            for mi in range(4):
                ti = blk * 4 + mi
                ps = fw.psum()
                for kt in range(KT):
                    fw.mm(ps[:, :n], lhsT=wb[:, kt, mi * 128:(mi + 1) * 128], rhs=hT[:, kt, :n],
                          start=(kt == 0), stop=(kt == KT - 1), sig=False)
                s = stg[si % 4]
                fw.copy(s[:, :n], ps[:, :n], eng=("act" if si % 2 else "dve"))
                si += 1
                if ti < 4:
                    dst = fm_rows(P.dr["Pq"], ti * 128, TL, t0, n)
                elif ti < 8:
                    dst = fm_rows(P.dr["Pk"], (ti - 4) * 128, TL, t0, n)
                elif ti < 28:
                    dst = fm_rows(P.dr["PD"], (ti - 8) * 128, TL, t0, n)
                else:
                    dst = fm_rows(P.dr["Pu"], (ti - 28) * 128, TL, t0, n)
                fw.dma(dst, s[:, :n], queue=("act" if si % 2 else "pool"))
        if suba == 2:
            continue
        for which in range(2):
            wb = wbs[wi % 3]
            wi += 1
            fw.dma(wb[:, :, :], wview(P, wname, 0, KT, 4096 + which * 512, 512))
            for st in range(n // 128):
                ps = fw.psum()
                for kt in range(KT):
                    fw.mm(ps[:, :512], lhsT=hT[:, kt, st * 128:(st + 1) * 128], rhs=wb[:, kt, :],
                          start=(kt == 0), stop=(kt == KT - 1), sig=False)
                tok0 = t0 + st * 128
                if which == 0:
                    s = stgb[si % 2]
                    fw.copy(s[:, :], ps[:, :512], eng=("act" if si % 2 else "dve"))
                    fw.dma(dram_ap(P.dr["Pv"], tok0 * 512, [[512, 128], [1, 512]]), s[:, :], queue=("act" if si % 2 else "pool"))
                else:
                    s = stg[si % 4]
                    fw.copy(s[:, :], ps[:, :512], eng=("act" if si % 2 else "dve"))
                    fw.dma(dram_ap(P.dr["Psv"], tok0 * 512, [[512, 128], [1, 512]]), s[:, :], queue=("act" if si % 2 else "pool"))
                si += 1
        for st in range(n // 128):
            ps = fw.psum()
            for kt in range(KT):
                fw.mm(ps[:, :16], lhsT=hT[:, kt, st * 128:(st + 1) * 128], rhs=wab[:, kt, :],
                      start=(kt == 0), stop=(kt == KT - 1))
            s = stg[si % 4]
            si += 1
            fw.copy(s[:, :16], ps[:, :16])
            tok0 = st * 128 + t0
            fw.dma(dram_ap(P.dr["Pab"], tok0 * 16, [[16, 128], [1, 16]]), s[:, 0:16], queue="pool")
    fw.pop_scope()


def phase_c(P, l):
    c, fw = P.cfg, P.fw
    D, KT, Th, L, TL, FT = c.D, c.KT, c.Th, c.L, c.TL, c.FT
    last = (l == c.NL - 1)
    sel = last and P.split
    if sel:
        Th2 = P.Th2
        fw.barrier()
        for nm, rows in (("xmy", D), ("hT", D), ("yT", 2048)):
            fw.dma(dram_ap(P.dr[nm + "_sel"], 0, [[Th2, 128], [128 * Th2, rows // 128], [1, Th2]]),
                   dram_ap(P.dr[nm], P.reg_half, [[TL, 128], [128 * TL, rows // 128], [1, Th2]]), queue="pool")
        fw.barrier()
        blocks = [(t, min(512, Th2 - t), False) for t in range(0, Th2, 512)]
        xsrc, hsrc, ysrc, rs = P.dr["xmy_sel"], P.dr["hT_sel"], P.dr["yT_sel"], Th2
    else:
        blocks = c.blocks
        xsrc, hsrc, ysrc, rs = P.dr["xmy"], P.dr["hT"], P.dr["yT"], TL
    fw.push_scope()
    xb = fw.sbuf("xbC", [128, KT, 512], F32)
    wbs = [fw.sbuf("wbC", [128, KT, 512], BF16) for _ in range(3)]
    wbr = [fw.sbuf("wbrC", [128, 4, 512], BF16) for _ in range(2)]
    stg = [fw.sbuf("stgC", [128, 512], F32) for _ in range(3)]
    pools = {"sq": [fw.sbuf("sqC", [128, 512], BF16) for _ in range(2)],
             "rt": fw.sbuf("rtC", [128, 512], F32), "rs": fw.sbuf("rsC", [128, 512], F32),
             "tmp": [fw.sbuf("tmpC", [128, 512], F32) for _ in range(2)]}
    wi = 0
    si = 0
    for bi, (t0, n, is_ctx) in enumerate(blocks):
        if is_ctx and last:
            continue
        r = 1 if is_ctx else 0
        if l == 0:
            src = (dram_ap(P.inp["ctxT"], 0, [[L, 128], [128 * L, KT], [1, n]]) if is_ctx else
                   dram_ap(P.inp["xh"], t0, [[Th, 128], [128 * Th, KT], [1, n]]))
        else:
            src = dram_ap(xsrc, t0, [[rs, 128], [128 * rs, KT], [1, n]])
        fw.dma(xb[:, :, :n], src, queue="act")
        fw.push_scope()
        hT = fw.sbuf("hTC", [128, KT, 512], BF16)
        yT = fw.sbuf("yTC", [128, 16, 512], BF16)
        mg = fw.sbuf("mg", [128, KT, 512], F32)
        mgb = fw.sbuf("mgb", [128, KT, 512], BF16)
        fw.dma(hT[:, :, :n], dram_ap(hsrc, t0, [[rs, 128], [128 * rs, KT], [1, n]]), queue="act")
        fw.dma(yT[:, :, :n], dram_ap(ysrc, t0, [[rs, 128], [128 * rs, 16], [1, n]]), queue="pool")
        for i in range(4):
            for mb in range(D // 512):
                wb = wbs[wi % 3]
                wr_ = wbr[wi % 2]
                wi += 1
                fw.dma(wb[:, :, :], wview(P, "w_gate%d_%d" % (l, i), 0, KT, mb * 512, 512))
                fw.dma(wr_[:, :, :], wview(P, "w_br%d_%d" % (l, i), 0, 4, mb * 512, 512))
                for mi in range(4):
                    m = mb * 4 + mi
                    psg = fw.psum()
                    for kt in range(KT):
                        fw.mm(psg[:, :n], lhsT=wb[:, kt, mi * 128:(mi + 1) * 128], rhs=hT[:, kt, :n],
                              start=(kt == 0), stop=(kt == KT - 1), sig=False)
                    psb = fw.psum()
                    for k4 in range(4):
                        fw.mm(psb[:, :n], lhsT=wr_[:, k4, mi * 128:(mi + 1) * 128], rhs=yT[:, i * 4 + k4, :n],
                              start=(k4 == 0), stop=(k4 == 3))
                    sg = stg[si % 3]
                    si += 1
                    o = (l * 4 + i) * KT + m
                    fw.act(sg[:, :n], psg[:, :n], AF.Sigmoid, bias=P.bgate[:, o:o + 1])
                    if i == 0:
                        fw.tt(mg[:, m, :n], sg[:, :n], psb[:, :n], ALU.mult, append=(m > 0))
                    else:
                        fw.tt(sg[:, :n], sg[:, :n], psb[:, :n], ALU.mult)
                        fw.tt(mg[:, m, :n], mg[:, m, :n], sg[:, :n], ALU.add, eng="pool", append=True)
        for kt in range(KT):
            fw.copy(mgb[:, kt, :n], mg[:, kt, :n], eng=("act" if kt % 2 else "pool"), append=(kt > 0))
        for mb in range(D // 512):
            wb = wbs[wi % 3]
            wi += 1
            fw.dma(wb[:, :, :], wview(P, "w_out%d" % l, 0, KT, mb * 512, 512))
            for mi in range(4):
                m = mb * 4 + mi
                ps = fw.psum()
                for kt in range(KT):
                    fw.mm(ps[:, :n], lhsT=wb[:, kt, mi * 128:(mi + 1) * 128], rhs=mgb[:, kt, :n],
                          start=(kt == 0), stop=(kt == KT - 1), sig=False)
                fw.stt(xb[:, m, :n], ps[:, :n], modp(P, l, r, 2, m), xb[:, m, :n], op0=ALU.mult, op1=ALU.add,
                       append=True)
        fw.pop_scope()
        fw.push_scope()
        h2 = fw.sbuf("h2", [128, KT, 512], BF16)
        gT = fw.sbuf("gT", [128, FT, 512], BF16)
        emit_norm(P, xb, n, l, r, 1, h2, pools)
        for fb in range(FT // 4):
            w1 = wbs[wi % 3]
            wi += 1
            w3 = wbs[wi % 3]
            wi += 1
            fw.dma(w1[:, :, :], wview(P, "w1_%d" % l, 0, KT, fb * 512, 512))
            fw.dma(w3[:, :, :], wview(P, "w3_%d" % l, 0, KT, fb * 512, 512))
            for fi in range(4):
                f = fb * 4 + fi
                p1 = fw.psum()
                for kt in range(KT):
                    fw.mm(p1[:, :n], lhsT=w1[:, kt, fi * 128:(fi + 1) * 128], rhs=h2[:, kt, :n],
                          start=(kt == 0), stop=(kt == KT - 1), sig=False)
                p3 = fw.psum()
                for kt in range(KT):
                    fw.mm(p3[:, :n], lhsT=w3[:, kt, fi * 128:(fi + 1) * 128], rhs=h2[:, kt, :n],
                          start=(kt == 0), stop=(kt == KT - 1), sig=False)
                sl = stg[si % 3]
                si += 1
                fw.act(sl[:, :n], p1[:, :n], AF.Silu)
                fw.tt(gT[:, f, :n], sl[:, :n], p3[:, :n], ALU.mult, append=(f > 0))
        KC = FT // 4
        for mb in range(D // 512):
            pss = [fw.psum() for _ in range(4)]
            for kc in range(4):
                wb = wbs[wi % 3]
                wi += 1
                fw.dma(wb[:, :KC, :], wview(P, "w2_%d" % l, kc * KC, KC, mb * 512, 512))
                for mi in range(4):
                    for kk in range(KC):
                        fw.mm(pss[mi][:, :n], lhsT=wb[:, kk, mi * 128:(mi + 1) * 128], rhs=gT[:, kc * KC + kk, :n],
                              start=(kc == 0 and kk == 0), stop=(kc == 3 and kk == KC - 1), sig=(kk == KC - 1))
            for mi in range(4):
                m = mb * 4 + mi
                fw.stt(xb[:, m, :n], pss[mi][:, :n], modp(P, l, r, 5, m), xb[:, m, :n], op0=ALU.mult, op1=ALU.add,
                       append=True)
        if last:
            dst = dram_ap(P.out, t0, [[P.Th2, 128], [128 * P.Th2, KT], [1, n]])
        else:
            dst = dram_ap(P.dr["xmy"], t0, [[TL, 128], [128 * TL, KT], [1, n]])
        fw.dma(dst, xb[:, :, :n], queue="act")
        fw.pop_scope()
    fw.pop_scope()


def mixer_sgu(P, l, need_ctx):
    c, fw = P.cfg, P.fw
    Th, L, TL = c.Th, c.L, c.TL
    fw.push_scope()
    w32 = fw.sbuf("sgw32", [128, 512], F32)
    fw.dma(w32[:, :], dram_ap(P.inp["sg_wT"], l * 512, [[c.NL * 512, 128], [1, 512]]))
    wTb = fw.sbuf("sgwb", [128, 512], BF16)
    fw.copy(wTb[:, :], w32[:, :])
    b32 = fw.sbuf("sgb32", [1, 512], F32)
    fw.dma(b32[:, :], dram_ap(P.inp["sg_b"], l * 512, [[c.NL * 512, 1], [1, 512]]))
    bb = fw.sbuf("sgbb", [1, 512], BF16)
    fw.copy(bb[:, :], b32[:, :])
    vnbc = fw.sbuf("vnbc", [128, 512], F32)
    fw.dma(vnbc[:, :], dram_ap(P.inp["sg_vn"], l * 512, [[0, 128], [1, 512]]))
    vb_ = [fw.sbuf("sgv", [128, 512], F32) for _ in range(2)]
    gb_ = [fw.sbuf("sgg", [128, 512], F32) for _ in range(2)]
    vnb_ = [fw.sbuf("sgvn", [128, 512], BF16) for _ in range(2)]
    ub_ = [fw.sbuf("sgu", [128, 4, 128], F32) for _ in range(2)]
    ug_ = [fw.sbuf("sgug", [128, 4, 128], F32) for _ in range(2)]
    ys_ = [fw.sbuf("sgy", [128, 4, 128], BF16) for _ in range(2)]
    st_ = [fw.sbuf("sgst", [128, 6], F32) for _ in range(2)]
    mv_ = [fw.sbuf("sgmv", [128, 2], F32) for _ in range(2)]
    rs_ = [fw.sbuf("sgrs", [128, 2], F32) for _ in range(2)]
    ntile = c.NT + (L // 128 if need_ctx else 0)
    for i in range(ntile):
        tok0 = i * 128
        v, g, vnb, u, ug, ys, st, mv, rs = (x[i % 2] for x in (vb_, gb_, vnb_, ub_, ug_, ys_, st_, mv_, rs_))
        fw.dma(v[:, :], dram_ap(P.dr["Psv"], tok0 * 512, [[512, 128], [1, 512]]))
        fw.dma(u[:, :, :], dram_ap(P.dr["Pu"], tok0, [[TL, 128], [128 * TL, 4], [1, 128]]), queue="pool")
        fw.act(g[:, :], v[:, :], AF.Gelu_apprx_tanh)
        fw.op("dve", lambda E: E.bn_stats(out=st.t[:, :], in_=g.t[:, :]), reads=[g[:, :]], writes=[st[:, :]])
        fw.op("dve", lambda E: E.bn_aggr(out=mv.t[:, :], in_=st.t[:, :]), reads=[st[:, :]], writes=[mv[:, :]])
        fw.act(rs[:, 0:1], mv[:, 1:2], AF.Sqrt, bias=1e-6)
        fw.recip(rs[:, 1:2], rs[:, 0:1])
        fw.ts(g[:, :], g[:, :], mv[:, 0:1], rs[:, 1:2], op0=ALU.subtract, op1=ALU.mult)
        fw.tt(vnb[:, :], g[:, :], vnbc[:, :], ALU.mult, eng="pool")
        fw.act(ug[:, :, :], u[:, :, :], AF.Gelu_apprx_tanh)
        for gi in range(4):
            ps = fw.psum()
            fw.mm(ps[:, :128], lhsT=vnb[:, gi * 128:(gi + 1) * 128], rhs=wTb[:, gi * 128:(gi + 1) * 128],
                  start=True, stop=False)
            fw.mm(ps[:, :128], lhsT=P.ones_b[0:1, :], rhs=bb[0:1, gi * 128:(gi + 1) * 128], start=False, stop=True)
            fw.tt(ys[:, gi, :], ug[:, gi, :], ps[:, :128], ALU.mult, append=(gi > 0))
        fw.dma(dram_ap(P.dr["yT"], 1536 * TL + tok0, [[TL, 128], [128 * TL, 4], [1, 128]]), ys[:, :, :])
    fw.pop_scope()


def fm_rmsnorm(P, dst, src, n, gain, scale_mean, post_scale, pools, eps=1e-6):
    fw = P.fw
    i = pools["i"]
    pools["i"] += 1
    sq = pools["sq"][i % 2]
    rt = pools["rt"][i % 2]
    fw.act(sq[:, :n], src, AF.Square)
    ps = fw.psum()
    fw.mm(ps[:, :n], lhsT=P.ones_f[:, :], rhs=sq[:, :n])
    k = 1.0 / (post_scale * post_scale)
    fw.act(rt[:, :n], ps[:, :n], AF.Sqrt, bias=eps * k, scale=scale_mean * k)
    fw.recip(rt[:, :n], rt[:, :n])
    if gain is None:
        fw.tt(dst, src, rt[:, :n], ALU.mult)
    else:
        fw.stt(dst, src, gain, rt[:, :n], op0=ALU.mult, op1=ALU.mult)


def mixer_na(P, l, need_ctx):
    c, fw = P.cfg, P.fw
    T, L, TL, NT = c.T, c.L, c.TL, c.NT
    NP = P.NPAT
    NC = L // 128
    scale = 128 ** -0.5
    fw.push_scope()
    pools = {"i": 0, "sq": [fw.sbuf("nsq", [128, 512], F32) for _ in range(2)],
             "rt": [fw.sbuf("nrt", [128, 512], F32) for _ in range(2)]}
    qraw = fw.sbuf("qraw", [128, TL], F32)
    kraw = fw.sbuf("kraw", [128, TL], F32)
    qn = fw.sbuf("qn", [128, TL], BF16)
    kn = fw.sbuf("kn", [128, TL], BF16)
    vall = fw.sbuf("vall", [128, TL // 128, 128], BF16)
    eraw = fw.sbuf("eraw", [128, NP, 128], F32)
    E = fw.sbuf("E", [128, NP, 128], BF16)
    yst = fw.sbuf("nay", [128, TL], BF16)
    pa_ = [fw.sbuf("pa", [128, 512], BF16) for _ in range(3)]
    pb_ = [fw.sbuf("pb", [128, 512], BF16) for _ in range(3)]
    rd_ = [fw.sbuf("rd", [128, 128], F32) for _ in range(2)]
    for h in range(4):
        fw.dma(qraw[:, :], fm_rows(P.dr["Pq"], h * 128, TL, 0, TL))
        fw.dma(kraw[:, :], fm_rows(P.dr["Pk"], h * 128, TL, 0, TL), queue="pool")
        fw.dma(vall[:, :, :], dram_ap(P.dr["Pv"], h * 128, [[512, 128], [128 * 512, TL // 128], [1, 128]]), queue="pool")
        fw.dma(eraw[:, :, :], dram_ap(P.inp["na_bias"], ((l * 4 + h) * NP) * 128 * 128, [[128, 128], [128 * 128, NP], [1, 128]]))
        for q0 in range(0, NP, 4):
            q1 = min(NP, q0 + 4)
            fw.act(E[:, q0:q1, :], eraw[:, q0:q1, :], AF.Exp, append=(q0 > 0))
        for t0 in range(0, TL, 512):
            n = min(512, TL - t0)
            fm_rmsnorm(P, qn[:, t0:t0 + n], qraw[:, t0:t0 + n], n, P.hnorm[:, l * 3 + 0:l * 3 + 1], 1.0 / 128, 1.0, pools)
            fm_rmsnorm(P, kn[:, t0:t0 + n], kraw[:, t0:t0 + n], n, P.hnorm[:, l * 3 + 1:l * 3 + 2], 1.0 / 128, 1.0, pools)
        nq = NT + (NC if need_ctx else 0)
        ctx_slots = [(NT + s_, None) for s_ in range(NC)]
        def reg(buf, si):
            return (buf[0] if si < 4 else buf[1])[:, (si % 4) * 128:(si % 4 + 1) * 128]

        def stage1(jj):
            is_c = jj >= NT
            qv = qn[:, jj * 128:(jj + 1) * 128]
            if is_c:
                slots = ctx_slots
            else:
                cls = na_class(jj, NT)
                slots = [(jj + dm, cls * 7 + dm + 3) for dm in range(-3, 4)
                         if 0 <= jj + dm < NT and na_needed(jj, dm, NT)] + ctx_slots
            pa = pa_[jj % 3]
            pb = pb_[jj % 3]
            psa = fw.psum()
            psb = fw.psum()
            for si, (sl, pat) in enumerate(slots):
                fw.mm(reg((psa, psb), si), lhsT=kn[:, sl * 128:(sl + 1) * 128], rhs=qv, start=True, stop=True)
            na = min(4, len(slots))
            fw.act(pa[:, :na * 128], psa[:, :na * 128], AF.Exp, scale=scale)
            if len(slots) > 4:
                nbk = len(slots) - 4
                fw.act(pb[:, :nbk * 128], psb[:, :nbk * 128], AF.Exp, scale=scale)
            for si, (sl, pat) in enumerate(slots):
                if pat is not None:
                    tgt = reg((pa, pb), si)
                    fw.tt(tgt, tgt, E[:, pat, :], ALU.mult, eng=("dve" if si % 2 else "pool"))
            return jj, slots, pa, pb

        def stage2(ctx_):
            jj, slots, pa, pb = ctx_
            pso = fw.psum()
            psd = fw.psum()
            for si, (sl, pat) in enumerate(slots):
                pv = reg((pa, pb), si)
                fw.mm(pso[:, :128], lhsT=vall[:, sl, :], rhs=pv, start=(si == 0), stop=(si == len(slots) - 1))
                fw.mm(psd[:, :128], lhsT=P.ones_b[:, :], rhs=pv, start=(si == 0), stop=(si == len(slots) - 1))
            rd = rd_[jj % 2]
            fw.recip(rd[:, :], psd[:, :128])
            fw.tt(yst[:, jj * 128:(jj + 1) * 128], pso[:, :128], rd[:, :], ALU.mult, append=(jj > 0))

        prev = None
        for jj in range(nq):
            cur = stage1(jj)
            if prev is not None:
                stage2(prev)
            prev = cur
        stage2(prev)
        fw.dma(fm_rows(P.dr["yT"], h * 128, TL, 0, nq * 128), yst[:, :nq * 128])
    fw.pop_scope()


def na_class(jj, NT):
    if jj < 2:
        return jj
    if jj >= NT - 2:
        return 3 + (jj - (NT - 2))
    return 2


def na_needed(jj, dm, NT):
    rows = 2 * NT
    kr_n = min(8, rows)
    for b in range(2):
        qr = 2 * jj + b
        s0 = min(max(qr - kr_n // 2, 0), rows - kr_n)
        for a in range(2):
            kr = 2 * (jj + dm) + a
            if s0 <= kr < s0 + kr_n:
                return True
    return False


def pd_row(typ, h):
    pr, hh = h // 2, h % 2
    return pr * PAIR_ROWS + typ * 256 + hh * 128


def mixer_ft(P, l, need_ctx):
    c, fw = P.cfg, P.fw
    L, T, NT, TL = c.L, c.T, c.NT, c.TL
    NTT = TL // 128
    fw.push_scope()
    cd32 = fw.sbuf("cd32", [128, 256], F32)
    fw.dma(cd32[:, :], P.inp["cdsd"].ap()[:, :])
    cdb = fw.sbuf("cdb", [128, 256], BF16)
    fw.copy(cdb[:, :], cd32[:, :])
    x32 = [fw.sbuf("ftx32", [128, 2048], F32) for _ in range(2)]
    xb = [fw.sbuf("ftxb", [128, TL], BF16) for _ in range(2)]
    AB = [fw.sbuf("ftAB", [128, NTT, 256], BF16) for _ in range(4)]
    ci = 0
    for g in range(4):
        row0 = pd_row(4, g)
        for s0 in range(0, TL, 2048):
            n = min(2048, TL - s0)
            xx = x32[ci % 2]
            ci += 1
            fw.dma(xx[:, :n], fm_rows(P.dr["PD"], row0, TL, s0, n), queue=("pool" if ci % 2 else "sp"))
            fw.copy(xb[g % 2][:, s0:s0 + n], xx[:, :n], eng=("pool" if ci % 2 else "dve"), append=(s0 > 0))
        for tt in range(NTT):
            ps = fw.psum()
            fw.mm(ps[:, :256], lhsT=xb[g % 2][:, tt * 128:(tt + 1) * 128], rhs=cdb[:, :])
            fw.copy(AB[g][:, tt, :], ps[:, :256], eng=("act" if tt % 2 else "dve"), append=(tt > 0))
    CB = 256
    cblk = [fw.sbuf("ftc", [128, NT, CB], BF16) for _ in range(2)]
    sblk = [fw.sbuf("fts", [128, NT, CB], BF16) for _ in range(2)]
    ysg = [fw.sbuf("fty", [128, CB], BF16) for _ in range(3)]
    yi = 0
    for cb in range(T // CB):
        cbf = cblk[cb % 2]
        sbf = sblk[cb % 2]
        fw.dma(cbf[:, :, :], dram_ap(P.W["ct"], cb * CB, [[T, 128], [128 * T, NT], [1, CB]]))
        fw.dma(sbf[:, :, :], dram_ap(P.W["st"], cb * CB, [[T, 128], [128 * T, NT], [1, CB]]), queue="pool")
        for g in range(4):
            ps = fw.psum()
            for tt in range(NT):
                fw.mm(ps[:, :CB], lhsT=AB[g][:, tt, 0:128], rhs=cbf[:, tt, :], start=(tt == 0), stop=False, sig=False)
                fw.mm(ps[:, :CB], lhsT=AB[g][:, tt, 128:256], rhs=sbf[:, tt, :], start=False, stop=(tt == NT - 1),
                      sig=False)
            ys = ysg[yi % 3]
            yi += 1
            fw.copy(ys[:, :], ps[:, :CB], eng=("act" if yi % 2 else "dve"))
            fw.dma(fm_rows(P.dr["yT"], 1024 + g * 128, TL, cb * CB, CB), ys[:, :])
    if need_ctx:
        nt = L // 128
        cc = fw.sbuf("ftcc", [128, nt, 2 * L], BF16)
        fw.dma(cc[:, :, :], dram_ap(P.inp["ctx_cs"], 0, [[2 * L, 128], [128 * 2 * L, nt], [1, 2 * L]]))
        for g in range(4):
            ps = fw.psum()
            for tt in range(nt):
                fw.mm(ps[:, :L], lhsT=AB[g][:, NT + tt, 0:128], rhs=cc[:, tt, 0:L], start=(tt == 0), stop=False)
                fw.mm(ps[:, :L], lhsT=AB[g][:, NT + tt, 128:256], rhs=cc[:, tt, L:2 * L], start=False,
                      stop=(tt == nt - 1))
            ys = ysg[yi % 3]
            yi += 1
            fw.copy(ys[:, :L], ps[:, :L])
            fw.dma(fm_rows(P.dr["yT"], 1024 + g * 128, TL, T, L), ys[:, :L])
    fw.pop_scope()


def bc_mid(v, n):
    a = v.ap
    pat = [list(x) for x in a.ap]
    return V(v.buf, bass.AP(a.tensor, a.offset, [pat[0], [0, n], pat[1]]))


def mixer_dn(P, l, need_ctx):
    c, fw = P.cfg, P.fw
    Th, L, T = c.Th, c.L, c.T
    W = T + L
    NTT = W // 128
    NC = L // 128
    TL = c.TL
    fw.push_scope()
    pools = {"i": 0, "sq": [fw.sbuf("dsq", [128, 512], F32) for _ in range(2)],
             "rt": [fw.sbuf("drt", [128, 512], F32) for _ in range(2)]}
    raws = [fw.sbuf("draw", [128, W], F32) for _ in range(2)]
    cvs = [fw.sbuf("dcv", [128, W], F32) for _ in range(2)]
    tst = [fw.sbuf("dtst", [128, 128], F32) for _ in range(3)]
    cw = fw.sbuf("dcw", [128, c.NL * 60], F32)
    fw.dma(cw[:, :], P.inp["dn_conv"].ap()[:, :])
    k_i = 0
    ti = 0
    for hh in range(4):
        for typ in range(3):
            raw = raws[k_i % 2]
            cv = cvs[k_i % 2]
            k_i += 1
            row0 = pd_row(typ, hh)
            fw.dma(raw[:, 0:L], fm_rows(P.dr["PD"], row0, TL, T, L))
            fw.dma(raw[:, L:], fm_rows(P.dr["PD"], row0, TL, 0, T), append=True, queue="pool")
            wo = ((l * 3 + typ) * 4 + hh) * 5
            for (s0, sl) in ((0, L), (L, T)):
                fw.ts(cv[:, s0:s0 + sl], raw[:, s0:s0 + sl], cw[:, wo + 2:wo + 3], None, op0=ALU.mult,
                      append=(s0 > 0))
                for j in (0, 1, 3, 4):
                    s = j - 2
                    lo = max(0, -s)
                    hi = sl - max(0, s)
                    eng = "dve" if j in (0, 3) else "pool"
                    fw.stt(cv[:, s0 + lo:s0 + hi], raw[:, s0 + lo + s:s0 + hi + s], cw[:, wo + j:wo + j + 1],
                           cv[:, s0 + lo:s0 + hi], op0=ALU.mult, op1=ALU.add, eng=eng, append=True)
            for t0 in range(0, W, 2048):
                n = min(2048, W - t0)
                fw.act(cv[:, t0:t0 + n], cv[:, t0:t0 + n], AF.Silu, append=True)
            if typ < 2:
                post = (128 ** -0.5) if typ == 0 else 1.0
                for t0 in range(0, W, 512):
                    n = min(512, W - t0)
                    fm_rmsnorm(P, cv[:, t0:t0 + n], cv[:, t0:t0 + n], n, None, 1.0, post, pools)
                name = "DNq" if typ == 0 else "DNk"
                fw.dma(dram_ap(P.dr[name], hh * 128 * W, [[W, 128], [1, W]]), cv[:, :])
            if typ >= 1:
                name = "DNkt" if typ == 1 else "DNvt"
                for tt in range(NTT):
                    ps = fw.psum()
                    fw.mm(ps[:, :128], lhsT=cv[:, tt * 128:(tt + 1) * 128], rhs=P.ident)
                    s = tst[ti % 3]
                    ti += 1
                    fw.copy(s[:, :], ps[:, :128], eng=("act" if ti % 2 else "dve"))
                    fw.dma(dram_ap(P.dr[name], (hh * W + tt * 128) * 128, [[128, 128], [1, 128]]), s[:, :],
                           queue=("pool" if ti % 2 else "sp"))
    fw.pop_scope()
    fw.push_scope()
    ab = fw.sbuf("dab", [128, NTT, 16], F32)
    fw.dma(ab[:, 0:NC, :], dram_ap(P.dr["Pab"], T * 16, [[16, 128], [128 * 16, NC], [1, 16]]))
    fw.dma(ab[:, NC:, :], dram_ap(P.dr["Pab"], 0, [[16, 128], [128 * 16, NTT - NC], [1, 16]]), append=True)
    par = fw.sbuf("dpar", [128, 16], F32)
    fw.dma(par[:, :], dram_ap(P.inp["dn_par"], l * 16, [[0, 128], [1, 16]]))
    nA = fw.sbuf("dnA", [128, 8], F32)
    fw.act(nA[:, :], par[:, 0:8], AF.Exp)
    fw.ts(nA[:, :], nA[:, :], -1.0, None, op0=ALU.mult)
    g = fw.sbuf("dg", [128, NTT, 8], F32)
    beta = fw.sbuf("dbeta", [128, NTT, 8], F32)
    fw.tt(g[:, :, :], ab[:, :, 0:8], bc_mid(par[:, 8:16], NTT), ALU.add)
    fw.act(g[:, :, :], g[:, :, :], AF.Exp)
    fw.act(g[:, :, :], g[:, :, :], AF.Ln, bias=1.0)
    fw.tt(g[:, :, :], g[:, :, :], bc_mid(nA[:, :], NTT), ALU.mult)
    fw.act(beta[:, :, :], ab[:, :, 8:16], AF.Sigmoid)
    gcl = fw.sbuf("dgcl", [128, NTT, 16], F32)
    for tt in range(NTT):
        ps = fw.psum()
        fw.mm(ps[:, 0:4], lhsT=P.Mf, rhs=g[:, tt, 0:4])
        fw.mm(ps[:, 4:8], lhsT=P.Mb, rhs=g[:, tt, 4:8])
        fw.mm(ps[:, 8:16], lhsT=P.Mch, rhs=g[:, tt, 0:8])
        fw.copy(gcl[:, tt, :], ps[:, 0:16], append=(tt > 0))
    egc = fw.sbuf("degc", [128, NTT, 8], F32)
    fw.act(egc[:, :, :], gcl[:, :, 0:8], AF.Exp)
    bg = fw.sbuf("dbg", [128, NTT, 8], F32)
    fw.tt(bg[:, :, :], beta[:, :, :], egc[:, :, :], ALU.mult)
    kdec = fw.sbuf("dkdec", [128, NTT, 8], F32)
    fw.tt(kdec[:, :, :], gcl[:, :, 8:16], gcl[:, :, 0:8], ALU.subtract)
    fw.act(kdec[:, :, :], kdec[:, :, :], AF.Exp)
    if getattr(c, "debug", ()):
        for nm, bf, k_ in (("g", g, 8), ("beta", beta, 8), ("gcl", gcl, 16), ("kdec", kdec, 8), ("bg", bg, 8)):
            o = P.nc.dram_tensor("dbg_" + nm, [128, NTT * k_], F32, kind="ExternalOutput")
            fw.dma(o.ap()[:, :], bf[:, :, :].rearrange("p a b -> p (a b)"))
    NBL = 8
    NBS = 10
    long_names = ["qT", "kT", "ktm", "vtm", "DmT", "EGB", "u0", "u1", "kd", "qkT", "qgT", "egl", "vnew", "oT"]
    pool_ = {nm: [fw.sbuf("m_" + nm, [128, 128], F32) for _ in range(NBL)] for nm in long_names}
    for nm in ("X", "Y", "P"):
        pool_[nm] = [fw.sbuf("m_" + nm, [128, 128], F32) for _ in range(NBS)]
    pool_["tmp"] = [fw.sbuf("m_tmp", [128, 128], F32) for _ in range(16)]
    cnt = {nm: 0 for nm in pool_}
    wT0 = [fw.sbuf("wT0", [128, 128], F32) for _ in range(NBL)]
    wT1 = [fw.sbuf("wT1", [128, 128], F32) for _ in range(NBL)]
    for b_ in wT0 + wT1:
        fw.memset(b_[:, :], 0.0)
    wcnt = [0]

    def nb(nm):
        b_ = pool_[nm][cnt[nm] % len(pool_[nm])]
        cnt[nm] += 1
        return b_

    S = {}
    for d in range(2):
        for hh in range(4):
            S[(d, hh)] = [fw.sbuf("S%d%d" % (d, hh), [128, 128], F32) for _ in range(2)]
            fw.memset(S[(d, hh)][0][:, :], 0.0)
    scnt = {k: 0 for k in S}
    order = {0: list(range(NTT)), 1: list(range(NC - 1, -1, -1)) + list(range(NTT - 1, NC - 1, -1))}
    qi = [0]

    def chain_tile(d, hh, tt):
        col = d * 4 + hh
        need_o = need_ctx or tt >= NC
        gc_col = gcl[:, tt, col:col + 1]
        kT, ktm, vtm = nb("kT"), nb("ktm"), nb("vtm")
        q_ = "sp" if qi[0] % 2 else "pool"
        qi[0] += 1
        fw.dma(kT[:, :], dram_ap(P.dr["DNk"], hh * 128 * W + tt * 128, [[W, 128], [1, 128]]), queue=q_)
        fw.dma(ktm[:, :], dram_ap(P.dr["DNkt"], (hh * W + tt * 128) * 128, [[128, 128], [1, 128]]), queue=q_)
        fw.dma(vtm[:, :], dram_ap(P.dr["DNvt"], (hh * W + tt * 128) * 128, [[128, 128], [1, 128]]), queue=q_)
        if need_o:
            qT = nb("qT")
            fw.dma(qT[:, :], dram_ap(P.dr["DNq"], hh * 128 * W + tt * 128, [[W, 128], [1, 128]]), queue=q_)
        yield
        psG = fw.psum()
        fw.mm(psG[:, :128], lhsT=kT[:, :], rhs=kT[:, :])
        diag = nb("tmp")
        fw.act(diag[:, :], P.ident, AF.Copy, scale=gc_col)
        psB = fw.psum()
        fw.mm(psB[:, :128], lhsT=P.ones_f[:, :], rhs=diag[:, :])
        mx = nb("tmp")
        fw.ts(mx[:, :], psB[:, :128], gc_col, 0.0, op0=ALU.subtract, op1=ALU.max)
        Dm = nb("tmp")
        fw.act(Dm[:, :], mx[:, :], AF.Exp, scale=-1.0)
        if need_o:
            m2 = nb("tmp")
            fw.ts(m2[:, :], psB[:, :128], gc_col, 0.0, op0=ALU.subtract, op1=ALU.min)
            DmT = nb("DmT")
            fw.act(DmT[:, :], m2[:, :], AF.Exp)
            EGB = nb("EGB")
            fw.act(EGB[:, :], psB[:, :128], AF.Exp)
        t1 = nb("tmp")
        fw.tt(t1[:, :], psG[:, :128], Dm[:, :], ALU.mult)
        A = nb("Y")
        fw.stt(A[:, :], t1[:, :], beta[:, tt, col:col + 1], P.ms[d], op0=ALU.mult, op1=ALU.mult)
        yield
        psU = fw.psum()
        fw.mm(psU[:, :128], lhsT=A[:, :], rhs=P.ident)
        U = nb("X")
        fw.copy(U[:, :], psU[:, :128], eng="act")
        Pm = nb("P")
        fw.tt(Pm[:, :], P.ident, U[:, :], ALU.subtract, eng="pool")
        yield
        X, Y = U, A
        for k in range(1, 6):
            psY = fw.psum()
            fw.mm(psY[:, :128], lhsT=X[:, :], rhs=Y[:, :])
            Y2 = nb("Y")
            fw.copy(Y2[:, :], psY[:, :128], eng="act")
            if k < 5:
                psX = fw.psum()
                fw.mm(psX[:, :128], lhsT=Y[:, :], rhs=X[:, :])
                X2 = nb("X")
                fw.copy(X2[:, :], psX[:, :128], eng="dve")
            yield
            psP = fw.psum()
            fw.mm(psP[:, :128], lhsT=Y2[:, :], rhs=Pm[:, :])
            P2 = nb("P")
            fw.tt(P2[:, :], Pm[:, :], psP[:, :128], ALU.add)
            Pm = P2
            Y = Y2
            if k < 5:
                X = X2
            yield
        TT_ = Pm
        vb, kbg, kd = nb("tmp"), nb("tmp"), nb("kd")
        fw.act(vb[:, :], vtm[:, :], AF.Copy, scale=beta[:, tt, col:col + 1])
        fw.act(kbg[:, :], ktm[:, :], AF.Copy, scale=bg[:, tt, col:col + 1])
        fw.ts(kd[:, :], ktm[:, :], kdec[:, tt, col:col + 1], None, op0=ALU.mult, eng="pool")
        psu = fw.psum()
        fw.mm(psu[:, :128], lhsT=TT_[:, :], rhs=vb[:, :])
        u0, u1 = nb("u0"), nb("u1")
        fw.ts(u0[:, :], psu[:, :128], P.ind2[:, 0:1], None, op0=ALU.mult)
        fw.ts(u1[:, :], psu[:, :128], P.ind2[:, 1:2], None, op0=ALU.mult)
        psw = fw.psum()
        fw.mm(psw[:, :128], lhsT=kbg[:, :], rhs=TT_[:, :])
        w0 = wT0[wcnt[0] % NBL]
        w1 = wT1[wcnt[0] % NBL]
        wcnt[0] += 1
        fw.copy(w0[:, 0:64], psw[:, 0:64], eng="act")
        fw.copy(w1[:, 64:128], psw[:, 64:128], eng="act")
        yield
        if need_o:
            psQ = fw.psum()
            fw.mm(psQ[:, :128], lhsT=kT[:, :], rhs=qT[:, :])
            qkT = nb("qkT")
            fw.tt(qkT[:, :], psQ[:, :128], DmT[:, :], ALU.mult)
            fw.tt(qkT[:, :], qkT[:, :], P.miT[d], ALU.mult, eng="pool")
            qgT = nb("qgT")
            fw.tt(qgT[:, :], qT[:, :], EGB[:, :], ALU.mult, eng="pool")
        gsel = nb("tmp")
        fw.ts(gsel[:, 0:2], P.ind2, g[:, tt, col:col + 1], None, op0=ALU.mult, eng="pool")
        psE = fw.psum()
        fw.mm(psE[:, 0:2], lhsT=P.ones_f[:, :], rhs=gsel[:, 0:2])
        egl = nb("egl")
        fw.act(egl[:, 0:2], psE[:, 0:2], AF.Exp)
        if need_o:
            oT = nb("oT")
        yield
        chunks = (0, 1) if d == 0 else (1, 0)
        for ci in chunks:
            c0 = ci * 64
            Sc = S[(d, hh)][scnt[(d, hh)] % 2]
            Sn = S[(d, hh)][(scnt[(d, hh)] + 1) % 2]
            scnt[(d, hh)] += 1
            wc = w0 if ci == 0 else w1
            uc = u0 if ci == 0 else u1
            ps1 = fw.psum()
            fw.mm(ps1[:, :128], lhsT=wc[:, :], rhs=Sc[:, :])
            vnew = nb("vnew")
            fw.tt(vnew[:, :], uc[:, :], ps1[:, :128], ALU.subtract)
            yield
            if need_o:
                pso = fw.psum()
                fw.mm(pso[:, 0:64], lhsT=Sc[:, :], rhs=qgT[:, c0:c0 + 64], start=True, stop=False)
                fw.mm(pso[:, 0:64], lhsT=vnew[:, :], rhs=qkT[:, c0:c0 + 64], start=False, stop=True)
                fw.copy(oT[:, c0:c0 + 64], pso[:, 0:64], eng="act", append=(ci != chunks[0]))
            psS = fw.psum()
            fw.mm(psS[:, :128], lhsT=kd[:, :], rhs=vnew[:, :])
            fw.stt(Sn[:, :], Sc[:, :], egl[:, ci:ci + 1], psS[:, :128], op0=ALU.mult, op1=ALU.add)
            yield
        if need_o:
            fw.dma(dram_ap(P.dr["DNo"], ((d * 4 + hh) * 128) * W + tt * 128, [[W, 128], [1, 128]]), oT[:, :],
                   queue="sp")

    for step in range(NTT):
        for d in range(2):
            alive = [chain_tile(d, hh, order[d][step]) for hh in range(4)]
            while alive:
                nxt = []
                for g_ in alive:
                    try:
                        next(g_)
                        nxt.append(g_)
                    except StopIteration:
                        pass
                alive = nxt
    fw.pop_scope()
    fw.push_scope()
    pools = {"i": 0, "sq": [fw.sbuf("psq", [128, 512], F32) for _ in range(2)],
             "rt": [fw.sbuf("prt", [128, 512], F32) for _ in range(2)]}
    of_ = [fw.sbuf("pof", [128, 512], F32) for _ in range(2)]
    ob_ = [fw.sbuf("pob", [128, 512], F32) for _ in range(2)]
    z_ = [fw.sbuf("pz", [128, 512], F32) for _ in range(2)]
    y_ = [fw.sbuf("py", [128, 512], BF16) for _ in range(2)]
    i = 0
    for hh in range(4):
        zrow = pd_row(3, hh)
        segs = []
        if need_ctx:
            segs.append((0, L, T))
        for s0 in range(0, T, 512):
            segs.append((L + s0, min(512, T - s0), s0))
        for (w0_, n, tcol) in segs:
            of, ob, z, y = of_[i % 2], ob_[i % 2], z_[i % 2], y_[i % 2]
            i += 1
            fw.dma(of[:, :n], dram_ap(P.dr["DNo"], ((0 * 4 + hh) * 128) * W + w0_, [[W, 128], [1, n]]))
            fw.dma(ob[:, :n], dram_ap(P.dr["DNo"], ((1 * 4 + hh) * 128) * W + w0_, [[W, 128], [1, n]]), queue="pool")
            fw.dma(z[:, :n], fm_rows(P.dr["PD"], zrow, TL, tcol, n))
            fw.tt(of[:, :n], of[:, :n], ob[:, :n], ALU.add, eng="pool")
            fm_rmsnorm(P, ob[:, :n], of[:, :n], n, P.hnorm[:, l * 3 + 2:l * 3 + 3], 1.0 / 128, 1.0, pools)
            fw.act(z[:, :n], z[:, :n], AF.Silu)
            fw.tt(y[:, :n], ob[:, :n], z[:, :n], ALU.mult)
            fw.dma(fm_rows(P.dr["yT"], 512 + hh * 128, TL, tcol, n), y[:, :n])
    fw.pop_scope()


def zero_missing(P):
    c, fw = P.cfg, P.fw
    rows = {"na": 0, "dn": 512, "ft": 1024, "sgu": 1536}
    miss = [m for m in rows if m not in c.mixers]
    if not miss:
        return
    fw.push_scope()
    z = fw.sbuf("zz", [128, c.TL], BF16)
    fw.memset(z[:, :], 0.0)
    for m in miss:
        for k in range(4):
            fw.dma(fm_rows(P.dr["yT"], rows[m] + k * 128, c.TL, 0, c.TL), z[:, :])
    fw.pop_scope()


def build(cfg):
    P = Prog(cfg)
    c, fw = cfg, P.fw
    D, L, TL, T = c.D, c.L, c.TL, c.T
    W = T + L
    P.dram("xmy", [D, TL]); P.dram("hT", [D, TL], BF16)
    P.dram("Pq", [512, TL]); P.dram("Pk", [512, TL]); P.dram("Pu", [512, TL]); P.dram("PD", [2 * PAIR_ROWS, TL])
    P.dram("Pv", [TL, 512], BF16); P.dram("Psv", [TL, 512]); P.dram("Pab", [TL, 16])
    P.dram("yT", [2048, TL], BF16)
    if P.split:
        P.dram("xmy_sel", [D, P.Th2]); P.dram("hT_sel", [D, P.Th2], BF16); P.dram("yT_sel", [2048, P.Th2], BF16)
    P.dram("DNq", [512, W]); P.dram("DNk", [512, W]); P.dram("DNkt", [4 * W, 128]); P.dram("DNvt", [4 * W, 128])
    P.dram("DNo", [1024, W])
    stage = getattr(c, "stage", 99)
    load_consts(P)
    compute_mod(P)
    for l in range(c.NL if stage >= 3 else 0):
        need_ctx = l < c.NL - 1
        phase_a(P, l)
        fw.barrier()
        if stage == 3:
            break
        if "sgu" in c.mixers:
            mixer_sgu(P, l, need_ctx)
        if "na" in c.mixers:
            mixer_na(P, l, need_ctx)
        if "ft" in c.mixers:
            mixer_ft(P, l, need_ctx)
        if "dn" in c.mixers:
            mixer_dn(P, l, need_ctx)
        zero_missing(P)
        fw.barrier()
        if stage == 5:
            break
        phase_c(P, l)
    fw.barrier()
    P.dbg = {}
    for nm in getattr(c, "debug", ()):
        t = P.dr[nm]
        o = P.nc.dram_tensor("dbg_" + nm, list(t.shape), t.dtype, kind="ExternalOutput")
        P.dbg[nm] = o
        fw.dma(o.ap()[:, :], t.ap()[:, :], queue="sp")
    fw.barrier()
    return P


def _bf16(a):
    return np.asarray(a, dtype=np.float32).astype(ml_dtypes.bfloat16)


def dft_consts(cfg):
    T, L = cfg.T, cfg.L
    d = np.arange(128)
    ang = 2 * np.pi * np.outer(d, d) / 128.0
    cdsd = np.concatenate([np.cos(ang), np.sin(ang)], 1) / np.sqrt(128.0)
    t = np.arange(T, dtype=np.int64)
    m = (np.outer(t, t) % T).astype(np.float64)
    ct = np.cos(2 * np.pi * m / T) / np.sqrt(T)
    st = -np.sin(2 * np.pi * m / T) / np.sqrt(T)
    tl = np.arange(L, dtype=np.int64)
    ml_ = (np.outer(tl, tl) % L).astype(np.float64)
    ccs = np.concatenate([np.cos(2 * np.pi * ml_ / L), -np.sin(2 * np.pi * ml_ / L)], 1) / np.sqrt(L)
    return cdsd.astype(np.float32), _bf16(ct), _bf16(st), _bf16(ccs)


def dn_consts():
    i = np.arange(128)[:, None]
    j = np.arange(128)[None, :]
    same = (i // 64) == (j // 64)
    ident = (i == j)
    Mf = same & (i <= j)
    Mb = same & (i >= j)
    Mch = same
    ms_f = same & (i > j)
    ms_b = same & (i < j)
    miT_f = same & (j >= i)
    miT_b = same & (j <= i)
    ind2 = np.zeros((128, 128), bool)
    ind2[:64, 0] = True
    ind2[64:, 1] = True
    return np.concatenate([x.astype(np.float32) for x in (ident, Mf, Mb, Mch, ms_f, ms_b, miT_f, miT_b, ind2)], 1)


def na_bias_tables(cfg, rpb):
    NL = rpb.shape[0]
    ROWS, NT = cfg.ROWS, cfg.NT
    out = np.full((NL, 4, 35, 128, 128), -30000.0, np.float32)
    col = np.arange(64)
    cstart = np.clip(col - 8, 0, 64 - 16)
    inwin = (col[None, :] >= cstart[:, None]) & (col[None, :] < cstart[:, None] + 16)
    dc = np.clip(col[None, :] - col[:, None], -15, 15) + 15
    kr_n = min(8, ROWS)
    for cls in range(5):
        if cls < 2:
            jj = cls
        elif cls == 2:
            jj = 2
        else:
            jj = NT - 2 + (cls - 3)
        if cls == 2 and NT < 5:
            continue
        j = jj
        for dm in range(-3, 4):
            m = j + dm
            if m < 0 or m >= NT:
                continue
            for a in range(2):
                for b in range(2):
                    kr, qr = 2 * m + a, 2 * j + b
                    s0 = int(np.clip(qr - kr_n // 2, 0, ROWS - kr_n))
                    if not (s0 <= kr < s0 + kr_n):
                        continue
                    blk = rpb[:, :, kr - qr + 7, :][:, :, dc]
                    blk = np.where(inwin[None, None], blk, -30000.0)
                    out[:, :, cls * 7 + dm + 3, a * 64:(a + 1) * 64, b * 64:(b + 1) * 64] = blk.transpose(0, 1, 3, 2)
    return out


def make_in_maps(cfg, inp):
    c = cfg
    D, KT, NL, L, T = c.D, c.KT, c.NL, c.L, c.T
    f32 = np.float32
    g = lambda k: np.asarray(inp[k], dtype=f32)
    perm = win_perm()
    cdsd, ct, st, ccs = dft_consts(c)
    shared = {}
    shared["w_in_s"] = np.ascontiguousarray(g("w_in")[:, :, perm]).reshape(NL * D, -1)
    shared["w_gate_s"] = g("w_gate").reshape(-1, D)
    shared["w_branch_s"] = g("w_branch").reshape(-1, D)
    shared["w_out_s"] = g("w_out").reshape(-1, D)
    shared["w1_s"] = g("ffn_w1").reshape(-1, c.DFF)
    shared["w3_s"] = g("ffn_w3").reshape(-1, c.DFF)
    shared["w2_s"] = g("ffn_w2").reshape(-1, D)
    shared["w_ada_s"] = g("w_ada").reshape(NL * D, 6 * D)
    nft = 6 * D // 128
    shared["b_ada_s"] = np.ascontiguousarray(g("b_ada").reshape(NL, nft, 128).transpose(2, 0, 1).reshape(128, -1))
    shared["normw"] = np.ascontiguousarray(
        np.stack([g("norm1_w"), g("norm2_w")], 1).reshape(NL, 2, KT, 128).transpose(3, 0, 1, 2).reshape(128, -1))
    shared["b_gate"] = np.ascontiguousarray(g("b_gate").reshape(NL, 4, KT, 128).transpose(3, 0, 1, 2).reshape(128, -1))
    shared["hnorm"] = np.ascontiguousarray(
        np.stack([g("na_qnorm"), g("na_knorm"), g("dn_onorm")], 1).transpose(2, 0, 1).reshape(128, -1))
    shared["na_bias"] = na_bias_tables(c, g("na_rpb")).reshape(-1, 128)
    cv = g("dn_conv").reshape(NL, 5, 3, 4, 128)
    shared["dn_conv"] = np.ascontiguousarray(cv.transpose(4, 0, 2, 3, 1).reshape(128, -1))
    shared["dn_par"] = np.ascontiguousarray(
        np.concatenate([g("dn_a_log").reshape(NL, 8), g("dn_dt_bias").reshape(NL, 8)], 1).reshape(1, -1))
    shared["sg_wT"] = np.ascontiguousarray(g("sg_w").transpose(3, 0, 1, 2).reshape(128, -1))
    shared["sg_b"] = g("sg_b").reshape(1, -1)
    shared["sg_vn"] = g("sg_vnorm").reshape(1, -1)
    shared["cdsd"] = cdsd
    shared["ct_s"] = ct
    shared["st_s"] = st
    shared["ctx_cs"] = ccs
    shared["consts"] = dn_consts()
    maps = []
    for core in range(c.ncores):
        b = core // c.cores_per_batch
        m = dict(shared)
        m["xh"] = np.ascontiguousarray(g("x")[b].T)
        m["ctxT"] = np.ascontiguousarray(g("ctx")[b].T)
        sc2 = np.stack([g("c")[b], g("c_ctx")], 0)
        m["scT"] = np.ascontiguousarray(sc2.reshape(2, KT, 128).transpose(2, 1, 0).reshape(128, KT * 2))
        p = core % c.cores_per_batch
        m["offs"] = np.array([[p * (T // c.cores_per_batch), 0, 0, 0]], np.int32)
        maps.append(m)
    return maps


_CACHE = {}


def run(cfg, inp):
    key = (cfg.D, cfg.ROWS, cfg.NL, tuple(cfg.mixers), cfg.ncores)
    if key not in _CACHE:
        _CACHE[key] = build(cfg)
    P = _CACHE[key]
    maps = make_in_maps(cfg, inp)
    res = run_bass_kernel_spmd(P.nc, maps, core_ids=list(range(cfg.ncores)))
    B = cfg.ncores // cfg.cores_per_batch
    global LAST_RES
    LAST_RES = res.results
    out = np.zeros((B, cfg.T, cfg.D), np.float32)
    Th2 = cfg.T // cfg.cores_per_batch
    for core in range(cfg.ncores):
        b, p = core // cfg.cores_per_batch, core % cfg.cores_per_batch
        out[b, p * Th2:(p + 1) * Th2] = res.results[core]["out"].T
    return out


def kernel(**inputs):
    cfg = Cfg()
    cfg.mixers = ("sgu", "na", "ft", "dn")
    cfg.ncores = 8
    cfg.cores_per_batch = 2
    return run(cfg, inputs)
```

```python
import numpy as np
import ml_dtypes
from contextlib import ExitStack
import concourse.bass as bass
import concourse.mybir as mybir
from concourse.bass_utils import run_bass_kernel_spmd

F32 = mybir.dt.float32
BF16 = mybir.dt.bfloat16
I32 = mybir.dt.int32
AF = mybir.ActivationFunctionType
ALU = mybir.AluOpType
AX = mybir.AxisListType
NCORES = 8
PAIRS = [[0, 1], [2, 3], [4, 5], [6, 7]]
ALL8 = [list(range(8))]
PAIRS_B = [[0, 4], [1, 5], [2, 6], [3, 7]]


class Buf:
    __slots__ = ("t", "w", "r", "name")

    def __init__(self, t, name):
        self.t = t
        self.name = name
        self.w = {}
        self.r = {}

    def __getitem__(self, idx):
        return V(self, self.t[idx])


class V:
    __slots__ = ("buf", "ap")

    def __init__(self, buf, ap):
        self.buf = buf
        self.ap = ap

    def __getitem__(self, idx):
        return V(self.buf, self.ap[idx])

    def rearrange(self, *a, **k):
        return V(self.buf, self.ap.rearrange(*a, **k))

    def to_broadcast(self, shape):
        return V(self.buf, self.ap.to_broadcast(shape))


def _ap(x):
    return x.ap if isinstance(x, V) else x


class Eng:
    def __init__(self, name, eng, sem, sid):
        self.name = name
        self.eng = eng
        self.sem = sem
        self.sid = sid
        self.count = 0
        self.known = {}


class FW:
    def __init__(self, nc, n_dma_ring=12):
        self.nc = nc
        self.es = ExitStack()
        self.scopes = []
        self.sems = {}
        self.engs = {}
        self._nsem = 0
        for name, eng in (("pe", nc.tensor), ("act", nc.scalar), ("dve", nc.vector),
                          ("pool", nc.gpsimd), ("sp", nc.sync)):
            sid, sem = self._new_sem("e_" + name)
            self.engs[name] = Eng(name, eng, sem, sid)
        self.rings = {}
        for q in ("sp", "pool", "act"):
            ring = []
            for i in range(n_dma_ring):
                sid, sem = self._new_sem("d_%s%d" % (q, i))
                ring.append([sid, sem, 0])
            self.rings[q] = [ring, 0]
        self.bar_sid, self.bar_sem = self._new_sem("bar")
        self.bar_n = 0
        self.cc_sid, self.cc_sem = self._new_sem("cc")
        self.cc_n = 0
        self.psum_bufs = []
        self.psum_i = 0
        self.n_ins = 0
        self._uid = 0

    def _new_sem(self, name):
        sem = self.es.enter_context(self.nc.semaphore(name))
        sid = self._nsem
        self._nsem += 1
        self.sems[sid] = sem
        return sid, sem

    def push_scope(self):
        self.scopes.append(ExitStack())

    def pop_scope(self):
        self.barrier()
        self.scopes.pop().close()

    def sbuf(self, name, shape, dtype):
        st = self.scopes[-1] if self.scopes else self.es
        self._uid += 1
        t = st.enter_context(self.nc.sbuf_tensor("%s_%d" % (name, self._uid), list(shape), dtype))
        return Buf(t, name)

    def init_psum(self, n=8):
        for i in range(n):
            t = self.es.enter_context(self.nc.psum_tensor("ps%d" % i, [128, 512], F32))
            self.psum_bufs.append(Buf(t, "ps%d" % i))

    def psum(self):
        b = self.psum_bufs[self.psum_i % len(self.psum_bufs)]
        self.psum_i += 1
        return b

    def _wait(self, e, deps):
        for sid, val in deps.items():
            if e.name == "pe" and sid == e.sid:
                continue
            if e.known.get(sid, 0) < val:
                e.eng.wait_ge(self.sems[sid], val)
                e.known[sid] = val

    @staticmethod
    def _merge(dst, src):
        for k, v in src.items():
            if dst.get(k, 0) < v:
                dst[k] = v

    def _collect(self, reads, writes):
        deps = {}
        for v in reads:
            if isinstance(v, V):
                self._merge(deps, v.buf.w)
        for v, append in writes:
            if isinstance(v, V):
                self._merge(deps, v.buf.r)
                if not append:
                    self._merge(deps, v.buf.w)
        return deps

    def _commit(self, sid, val, reads, writes):
        for v in reads:
            if isinstance(v, V):
                b = v.buf
                if b.r.get(sid, 0) < val:
                    b.r[sid] = val
        for v, append in writes:
            if isinstance(v, V):
                b = v.buf
                if not append:
                    b.w = {sid: val}
                    b.r = {}
                elif b.w.get(sid, 0) < val:
                    b.w[sid] = val

    def op(self, eng, fn, reads=(), writes=(), append=False, signal=True):
        e = self.engs[eng]
        wr = [(w, append) for w in writes]
        self._wait(e, self._collect(reads, wr))
        ins = fn(e.eng)
        self.n_ins += 1
        if signal:
            e.count += 1
            ins.then_inc(e.sem, 1)
            val = e.count
        else:
            val = e.count + 1
        self._commit(e.sid, val, reads, wr)
        return ins

    def dma(self, out, in_, queue="sp", append=False, **kw):
        e = self.engs[queue]
        ring, idx = self.rings[queue]
        slot = ring[idx % len(ring)]
        self.rings[queue][1] = idx + 1
        reads = [in_]
        wr = [(out, append)]
        deps = self._collect(reads, wr)
        if slot[2] > 0:
            self._merge(deps, {slot[0]: slot[2]})
        self._wait(e, deps)
        ins = e.eng.dma_start(out=_ap(out), in_=_ap(in_), **kw)
        slot[2] += 16
        ins.then_inc(slot[1], 16)
        self.n_ins += 1
        self._commit(slot[0], slot[2], reads, wr)
        return ins

    def barrier(self):
        sp = self.engs["sp"]
        deps = {}
        for e in self.engs.values():
            if e.count > 0:
                deps[e.sid] = e.count
        for q, (ring, _) in self.rings.items():
            for slot in ring:
                if slot[2] > 0:
                    deps[slot[0]] = slot[2]
        if self.cc_n > 0:
            deps[self.cc_sid] = self.cc_n
        self._wait(sp, deps)
        self.bar_n += 1
        sp.eng.sem_inc(self.bar_sem, 1)
        for e in self.engs.values():
            if e is not sp:
                e.eng.wait_ge(self.bar_sem, self.bar_n)
            for sid, val in deps.items():
                e.known[sid] = max(e.known.get(sid, 0), val)

    def cc(self, kind, groups, src, dst):
        pool = self.engs["pool"]
        ins = pool.eng.collective_compute(kind, ALU.bypass, replica_groups=groups, ins=[src], outs=[dst])
        self.cc_n += 1
        ins.then_inc(self.cc_sem, 1)
        pool.eng.wait_ge(self.cc_sem, self.cc_n)
        pool.known[self.cc_sid] = self.cc_n

    def collective(self, kind, groups, src, dst):
        self.barrier()
        self.cc(kind, groups, src, dst)
        self.barrier()

    def act(self, out, in_, func, bias=0.0, scale=1.0, append=False):
        return self.op("act", lambda E: E.activation(out=_ap(out), in_=_ap(in_), func=func, bias=_ap(bias),
                                                     scale=_ap(scale)),
                       reads=[in_, bias, scale], writes=[out], append=append)

    def tt(self, out, in0, in1, op, eng="dve", append=False):
        return self.op(eng, lambda E: E.tensor_tensor(out=_ap(out), in0=_ap(in0), in1=_ap(in1), op=op),
                       reads=[in0, in1], writes=[out], append=append)

    def ts(self, out, in0, s1, s2=None, op0=ALU.mult, op1=None, eng="dve", append=False):
        kw = {}
        if op1 is not None:
            kw["op1"] = op1
        return self.op(eng, lambda E: E.tensor_scalar(out=_ap(out), in0=_ap(in0), scalar1=_ap(s1),
                                                      scalar2=_ap(s2), op0=op0, **kw),
                       reads=[in0, s1, s2], writes=[out], append=append)

    def stt(self, out, in0, scalar, in1, op0, op1, eng="dve", append=False):
        eng = "dve"
        return self.op(eng, lambda E: E.scalar_tensor_tensor(out=_ap(out), in0=_ap(in0), scalar=_ap(scalar),
                                                             in1=_ap(in1), op0=op0, op1=op1),
                       reads=[in0, scalar, in1], writes=[out], append=append)

    def copy(self, out, in_, eng="dve", append=False):
        if eng == "act":
            return self.act(out, in_, AF.Copy, append=append)
        return self.op(eng, lambda E: E.tensor_copy(out=_ap(out), in_=_ap(in_)),
                       reads=[in_], writes=[out], append=append)

    def recip(self, out, in_, append=False):
        return self.op("dve", lambda E: E.reciprocal(out=_ap(out), in_=_ap(in_)),
                       reads=[in_], writes=[out], append=append)

    def memset(self, out, val, eng="pool", append=False):
        return self.op(eng, lambda E: E.memset(_ap(out), val), reads=[], writes=[out], append=append)

    def mm(self, out, lhsT, rhs, start=True, stop=True, sig=True):
        return self.op("pe", lambda E: E.matmul(_ap(out), lhsT=_ap(lhsT), rhs=_ap(rhs), start=start, stop=stop),
                       reads=[lhsT, rhs], writes=[out], append=not start, signal=(sig or stop))


class Cfg:
    def __init__(self, D=2048, ROWS=64, NL=2, L=256):
        self.D = D
        self.KT = D // 128
        self.ROWS = ROWS
        self.T = ROWS * 64
        self.Th = self.T
        self.L = L
        self.TL = self.Th + L
        self.NL = NL
        self.DFF = ((8 * D + 3 * 256 - 1) // (3 * 256)) * 256
        self.FT = self.DFF // 128
        self.DIN = 5136
        self.ncores = 8
        self.cores_per_batch = 2
        self.NTh = self.Th // 128
        self.NT = self.T // 128
        self.blocks = []
        t = 0
        while t < self.Th:
            n = min(512, self.Th - t)
            self.blocks.append((t, n, False))
            t += n
        self.blocks.append((self.Th, L, True))


def win_perm():
    o = {"na_q": 0, "na_k": 512, "na_v": 1024, "dn_q": 1536, "dn_k": 2048, "dn_v": 2560, "dn_z": 3072,
         "dn_a": 3584, "dn_b": 3592, "ft": 3600, "sg_u": 4112, "sg_v": 4624}
    cols = []
    cols += list(range(o["na_q"], o["na_q"] + 512))
    cols += list(range(o["na_k"], o["na_k"] + 512))
    for pr in range(2):
        for typ in ("dn_q", "dn_k", "dn_v", "dn_z", "ft"):
            for hh in range(2):
                h = 2 * pr + hh
                cols += list(range(o[typ] + h * 128, o[typ] + (h + 1) * 128))
    cols += list(range(o["sg_u"], o["sg_u"] + 512))
    nfm = len(cols)
    cols += list(range(o["na_v"], o["na_v"] + 512))
    cols += list(range(o["sg_v"], o["sg_v"] + 512))
    cols += list(range(o["dn_a"], o["dn_a"] + 16))
    assert len(cols) == 5136 and nfm == 4096
    return np.array(cols)


PAIR_ROWS = 1280


def dram_ap(t, offset, pat):
    return bass.AP(t, offset, [list(x) for x in pat])


class Prog:
    def __init__(self, cfg):
        self.cfg = cfg
        c = cfg
        self.nc = nc = bass.Bass("TRN2", target_bir_lowering=False)
        self.fw = fw = FW(nc)
        fw.init_psum()
        self.inp = {}
        self.dr = {}
        D, KT, NL, Th, L, TL, T = c.D, c.KT, c.NL, c.Th, c.L, c.TL, c.T
        NA8 = 6 * D
        self.NA8 = NA8
        self.NPAT = 35
        I = self._in
        I("xh", [D, Th]); I("ctxT", [D, L])
        I("scT", [128, KT * 2])
        I("w_ada_s", [NL * D, NA8]); I("b_ada_s", [128, NL * (NA8 // 128)])
        I("normw", [128, NL * 2 * KT]); I("b_gate", [128, NL * 4 * KT])
        I("w_in_s", [NL * D, c.DIN]); I("w_gate_s", [NL * 4 * D, D])
        I("w_branch_s", [NL * 4 * 512, D]); I("w_out_s", [NL * D, D])
        I("w1_s", [NL * D, c.DFF]); I("w3_s", [NL * D, c.DFF]); I("w2_s", [NL * c.DFF, D])
        I("hnorm", [128, NL * 3])
        I("na_bias", [NL * 4 * self.NPAT * 128, 128])
        I("dn_conv", [128, NL * 3 * 4 * 5])
        I("dn_par", [1, NL * 16])
        I("sg_wT", [128, NL * 4 * 128]); I("sg_b", [1, NL * 512]); I("sg_vn", [1, NL * 512])
        I("cdsd", [128, 256])
        I("ct_s", [T, T], BF16); I("st_s", [T, T], BF16)
        I("ctx_cs", [L, 2 * L], BF16)
        I("consts", [128, 9 * 128])
        self.split = getattr(cfg, "cores_per_batch", 1) == 2
        self.Th2 = Th // 2 if self.split else Th
        I("offs", [1, 4], I32)
        self.out = nc.dram_tensor("out", [D, self.Th2], F32, kind="ExternalOutput")

    def _in(self, name, shape, dt=F32):
        self.inp[name] = self.nc.dram_tensor(name, list(shape), dt, kind="ExternalInput")

    def dram(self, name, shape, dt=F32):
        t = self.nc.dram_tensor(name, list(shape), dt)
        self.dr[name] = t
        return t


def cast_and_gather(P, src_t, row0, nr, M, dst_name, chunk=4096):
    fw = P.fw
    KTt = nr // 128
    nb = M // 512
    rem = M - nb * 512
    full = P.dram(dst_name, [nb * 128, KTt * 512], BF16)
    P.Wmeta[dst_name] = (KTt, nb, rem)
    i = P.cast_i
    kstep = max(1, 4096 // 512)
    for mb in range(nb):
        for k0 in range(0, KTt, kstep):
            nk = min(kstep, KTt - k0)
            a = P.cast_in[i % len(P.cast_in)]
            b = P.cast_out[i % len(P.cast_out)]
            src = dram_ap(src_t, (row0 + k0 * 128) * M + mb * 512, [[M, 128], [128 * M, nk], [1, 512]])
            fw.dma(a[:, :nk * 512].rearrange("p (k m) -> p k m", k=nk), src, queue="sp")
            fw.copy(b[:, :nk * 512], a[:, :nk * 512], eng=("dve", "act")[i % 2])
            fw.dma(dram_ap(full, (mb * 128) * KTt * 512 + k0 * 512, [[KTt * 512, 128], [1, nk * 512]]), b[:, :nk * 512],
                   queue="pool")
            i += 1
    if rem:
        remt = P.dram(dst_name + "_rem", [128, KTt * rem], BF16)
        P.W[dst_name + "_rem"] = remt
        a = P.cast_in[i % len(P.cast_in)]
        b = P.cast_out[i % len(P.cast_out)]
        src = dram_ap(src_t, row0 * M + nb * 512, [[M, 128], [128 * M, KTt], [1, rem]])
        fw.dma(a[:, :KTt * rem].rearrange("p (k m) -> p k m", k=KTt), src)
        fw.copy(b[:, :KTt * rem], a[:, :KTt * rem])
        fw.dma(remt.ap()[:, :], b[:, :KTt * rem], queue="pool")
        i += 1
    P.cast_i = i
    return full, None, full


def prep_weights(P):
    c, fw = P.cfg, P.fw
    D, NL = c.D, c.NL
    fw.push_scope()
    P.cast_in = [fw.sbuf("cin", [128, 4096], F32) for _ in range(3)]
    P.cast_out = [fw.sbuf("cout", [128, 4096], BF16) for _ in range(3)]
    P.W = {}
    P.Wmeta = {}
    P.cast_i = 0
    todo = []
    r8 = D
    for l in range(NL):
        todo.append(("w_in%d" % l, P.inp["w_in_s"], l * r8, r8, c.DIN))
        for i in range(4):
            todo.append(("w_gate%d_%d" % (l, i), P.inp["w_gate_s"], (l * 4 + i) * r8, r8, D))
            todo.append(("w_br%d_%d" % (l, i), P.inp["w_branch_s"], (l * 4 + i) * 512, 512, D))
        todo.append(("w_out%d" % l, P.inp["w_out_s"], l * r8, r8, D))
        todo.append(("w1_%d" % l, P.inp["w1_s"], l * r8, r8, c.DFF))
        todo.append(("w3_%d" % l, P.inp["w3_s"], l * r8, r8, c.DFF))
        todo.append(("w2_%d" % l, P.inp["w2_s"], l * c.DFF, c.DFF, D))
    pairs = []
    for name, src, row0, nr, M in todo:
        send, mid, full = cast_and_gather(P, src, row0, nr, M, name)
        P.W[name] = full
        pairs.append((send, mid, full))
    P.W["ct"] = P.inp["ct_s"]
    P.W["st"] = P.inp["st_s"]
    fw.pop_scope()


def load_consts(P):
    c, fw = P.cfg, P.fw
    NL, KT = c.NL, c.KT
    P.cst = fw.sbuf("consts", [128, 9 * 128], F32)
    fw.dma(P.cst[:, :], P.inp["consts"].ap()[:, :])
    k = lambda i: P.cst[:, i * 128:(i + 1) * 128]
    P.ident, P.Mf, P.Mb, P.Mch = k(0), k(1), k(2), k(3)
    P.ms = [k(4), k(5)]
    P.miT = [k(6), k(7)]
    P.ind2 = P.cst[:, 8 * 128:8 * 128 + 2]
    P.ones_f = fw.sbuf("ones_f", [128, 128], F32)
    fw.memset(P.ones_f[:, :], 1.0)
    P.ones_b = fw.sbuf("ones_b", [128, 128], BF16)
    fw.memset(P.ones_b[:, :], 1.0)
    P.ident_b = fw.sbuf("ident_b", [128, 128], BF16)
    fw.copy(P.ident_b[:, :], P.ident)
    P.normw = fw.sbuf("normw", [128, NL * 2 * KT], F32)
    fw.dma(P.normw[:, :], P.inp["normw"].ap()[:, :])
    P.bgate = fw.sbuf("bgate", [128, NL * 4 * KT], F32)
    fw.dma(P.bgate[:, :], P.inp["b_gate"].ap()[:, :])
    P.hnorm = fw.sbuf("hnorm", [128, NL * 3], F32)
    fw.dma(P.hnorm[:, :], P.inp["hnorm"].ap()[:, :])
    P.offs_sb = fw.sbuf("offs", [1, 4], I32)
    fw.dma(P.offs_sb[:, :], P.inp["offs"].ap()[:, :], queue="pool")
    fw.barrier()
    P.reg_half = fw.es.enter_context(P.nc.gpsimd.register("dynhalf"))
    P.nc.gpsimd.reg_load(P.reg_half, P.offs_sb.t[:1, 0:1])
    fw.barrier()


def compute_mod(P):
    c, fw, nc = P.cfg, P.fw, P.nc
    D, KT, NL = c.D, c.KT, c.NL
    NA8 = P.NA8
    nft = NA8 // 128
    NJ = 6 * KT
    P.mod = fw.sbuf("modmy", [128, NL * NJ * 2], F32)
    P.modp = fw.sbuf("modp", [128, NL * 2 * 6 * KT], F32)
    fw.push_scope()
    sc = fw.sbuf("sc", [128, KT * 2], F32)
    fw.dma(sc[:, :], P.inp["scT"].ap()[:, :])
    scs = fw.sbuf("scs", [128, KT * 2], F32)
    fw.act(scs[:, :], sc[:, :], AF.Silu)
    bsh = fw.sbuf("bsh", [128, NL * nft], F32)
    fw.dma(bsh[:, :], P.inp["b_ada_s"].ap()[:, :])
    msh = fw.sbuf("msh", [128, NL * nft * 8], F32)
    fw.memset(msh[:, :], 0.0)
    was = [fw.sbuf("wa", [128, KT * 128], F32) for _ in range(2)]
    i = 0
    for l in range(NL):
        for ft in range(nft):
            wa = was[i % 2]
            i += 1
            src = dram_ap(P.inp["w_ada_s"], l * D * NA8 + ft * 128, [[NA8, 128], [128 * NA8, KT], [1, 128]])
            fw.dma(wa[:, :].rearrange("p (k m) -> p k m", k=KT), src)
            ps = fw.psum()
            for kt in range(KT):
                fw.mm(ps[:, 0:2], lhsT=wa[:, kt * 128:(kt + 1) * 128], rhs=scs[:, kt * 2:(kt + 1) * 2],
                      start=(kt == 0), stop=(kt == KT - 1))
            j = l * nft + ft
            fw.act(msh[:, j * 8:j * 8 + 2], ps[:, 0:2], AF.Identity, bias=bsh[:, j:j + 1], append=True)
    import os as _os
    sub = int(_os.environ.get("SUB", "9"))
    if sub == 0:
        fw.pop_scope()
        return
    msend = P.dram("mod_snd", [NL * nft * 128, 8])
    mfull = P.dram("mod_full", [2 * NL * nft * 128, 8])
    fw.dma(dram_ap(msend, 0, [[8, 128], [128 * 8, NL * nft], [1, 8]]),
           msh[:, :].rearrange("p (j c) -> p j c", c=8))
    if sub == 1:
        fw.pop_scope()
        return
    fw.barrier()
    mfull = msend
    if sub == 2:
        fw.pop_scope()
        return
    NJ = 6 * KT
    mall = fw.sbuf("mall", [128, NL * NJ * 8], F32)
    for l in range(NL):
        for r in range(1):
            src = dram_ap(mfull, ((r * NL + l) * nft) * 128 * 8, [[8, 128], [128 * 8, nft], [1, 8]])
            dst = mall[:, (l * NJ + r * nft) * 8:(l * NJ + (r + 1) * nft) * 8].rearrange("p (j c) -> p j c", c=8)
            fw.dma(dst, src)
    mv = mall[:, :].rearrange("p (j c) -> p j c", c=8)
    mm_ = P.mod[:, :].rearrange("p (j r) -> p j r", r=2)
    fw.copy(mm_[:, :, :], mv[:, :, 0:2])
    for l in range(NL):
        for r in range(2):
            def chunk(k):
                return mm_[:, l * NJ + k * KT:l * NJ + (k + 1) * KT, r]

            def dst(k):
                o = ((l * 2 + r) * 6 + k) * KT
                return P.modp[:, o:o + KT]
            nw1 = P.normw[:, (l * 2 + 0) * KT:(l * 2 + 1) * KT]
            nw2 = P.normw[:, (l * 2 + 1) * KT:(l * 2 + 2) * KT]
            fw.stt(dst(0), chunk(1), 1.0, nw1, op0=ALU.add, op1=ALU.mult)
            fw.copy(dst(1), chunk(0))
            fw.copy(dst(2), chunk(2))
            fw.stt(dst(3), chunk(4), 1.0, nw2, op0=ALU.add, op1=ALU.mult)
            fw.copy(dst(4), chunk(3))
            fw.copy(dst(5), chunk(5))
    fw.pop_scope()


def FWKEEP(P, name, shape, dt=F32):
    fw = P.fw
    saved = fw.scopes
    fw.scopes = []
    b = fw.sbuf(name, shape, dt)
    fw.scopes = saved
    return b


def modp(P, l, r, k, kt):
    c = P.cfg
    o = ((l * 2 + r) * 6 + k) * c.KT + kt
    return P.modp[:, o:o + 1]


def emit_norm(P, xb, n, l, r, which, hT, pools):
    c, fw = P.cfg, P.fw
    KT = c.KT
    ps = fw.psum()
    for kt in range(KT):
        sq = pools["sq"][kt % 2]
        fw.act(sq[:, :n], xb[:, kt, :n], AF.Square)
        fw.mm(ps[:, :n], lhsT=P.ones_b[:, :], rhs=sq[:, :n], start=(kt == 0), stop=(kt == KT - 1))
    rt = pools["rt"]
    fw.act(rt[:, :n], ps[:, :n], AF.Sqrt, bias=1e-6, scale=1.0 / c.D)
    rs = pools["rs"]
    fw.recip(rs[:, :n], rt[:, :n])
    ks, kb = (0, 1) if which == 0 else (3, 4)
    for kt in range(KT):
        tmp = pools["tmp"][kt % 2]
        fw.stt(tmp[:, :n], xb[:, kt, :n], modp(P, l, r, ks, kt), rs[:, :n], op0=ALU.mult, op1=ALU.mult)
        fw.act(hT[:, kt, :n], tmp[:, :n], AF.Identity, bias=modp(P, l, r, kb, kt), append=(kt > 0))


def wview(P, name, k0, nk, c0, nc_):
    KTt, nb, rem = P.Wmeta[name]
    mb = c0 // 512
    if mb < nb:
        assert nc_ == 512
        return dram_ap(P.W[name], (mb * 128) * KTt * 512 + k0 * 512, [[KTt * 512, 128], [512, nk], [1, 512]])
    assert nc_ == rem
    return dram_ap(P.W[name + "_rem"], k0 * rem, [[KTt * rem, 128], [rem, nk], [1, rem]])


def fm_rows(t, row0, ncols_total, col0, n, nrows=128):
    return dram_ap(t, row0 * ncols_total + col0, [[ncols_total, nrows], [1, n]])


def phase_a(P, l):
    c, fw = P.cfg, P.fw
    D, KT, Th, L, TL = c.D, c.KT, c.Th, c.L, c.TL
    fw.push_scope()
    xbs = [fw.sbuf("xb", [128, KT, 512], F32) for _ in range(1)]
    hTs = [fw.sbuf("hT", [128, KT, 512], BF16) for _ in range(2)]
    pools = {"sq": [fw.sbuf("sq", [128, 512], BF16) for _ in range(2)],
             "rt": fw.sbuf("rt", [128, 512], F32), "rs": fw.sbuf("rs", [128, 512], F32),
             "tmp": [fw.sbuf("tmp", [128, 512], F32) for _ in range(2)]}
    wbs = [fw.sbuf("wb", [128, KT, 512], BF16) for _ in range(3)]
    wab = fw.sbuf("wab", [128, KT, 16], BF16)
    stg = [fw.sbuf("stg", [128, 512], F32) for _ in range(4)]
    stgb = [fw.sbuf("stgb", [128, 512], BF16) for _ in range(2)]
    wname = "w_in%d" % l
    fw.dma(wab[:, :, :], wview(P, wname, 0, KT, 5120, 16))
    wi = 0
    si = 0
    for bi, (t0, n, is_ctx) in enumerate(c.blocks):
        r = 1 if is_ctx else 0
        xb = xbs[bi % len(xbs)]
        hT = hTs[bi % 2]
        if l == 0:
            src = (dram_ap(P.inp["ctxT"], 0, [[L, 128], [128 * L, KT], [1, n]]) if is_ctx else
                   dram_ap(P.inp["xh"], t0, [[Th, 128], [128 * Th, KT], [1, n]]))
        else:
            src = dram_ap(P.dr["xmy"], t0, [[TL, 128], [128 * TL, KT], [1, n]])
        fw.dma(xb[:, :, :n], src, queue="act")
        import os as _os
        suba = int(_os.environ.get("SUBA", "9"))
        emit_norm(P, xb, n, l, r, 0, hT, pools)
        if suba == 0:
            continue
        fw.dma(dram_ap(P.dr["hT"], t0, [[TL, 128], [128 * TL, KT], [1, n]]), hT[:, :, :n], queue="pool")
        if suba == 1:
            continue
        for blk in range(8):
            wb = wbs[wi % 3]
            wi += 1
            fw.dma(wb[:, :, :], wview(P, wname, 0, KT, blk * 512, 512))
            for mi in range(4):
                ti = blk * 4 + mi
                ps = fw.psum()
                for kt in range(KT):
                    fw.mm(ps[:, :n], lhsT=wb[:, kt, mi * 128:(mi + 1) * 128], rhs=hT[:, kt, :n],
                          start=(kt == 0), stop=(kt == KT - 1), sig=False)
                s = stg[si % 4]
                fw.copy(s[:, :n], ps[:, :n], eng=("act" if si % 2 else "dve"))
                si += 1
                if ti < 4:
                    dst = fm_rows(P.dr["Pq"], ti * 128, TL, t0, n)
                elif ti < 8:
                    dst = fm_rows(P.dr["Pk"], (ti - 4) * 128, TL, t0, n)
                elif ti < 28:
                    dst = fm_rows(P.dr["PD"], (ti - 8) * 128, TL, t0, n)
                else:
                    dst = fm_rows(P.dr["Pu"], (ti - 28) * 128, TL, t0, n)
                fw.dma(dst, s[:, :n], queue=("act" if si % 2 else "pool"))
        if suba == 2:
            continue
        for which in range(2):
            wb = wbs[wi % 3]
            wi += 1
            fw.dma(wb[:, :, :], wview(P, wname, 0, KT, 4096 + which * 512, 512))
            for st in range(n // 128):
                ps = fw.psum()
                for kt in range(KT):
                    fw.mm(ps[:, :512], lhsT=hT[:, kt, st * 128:(st + 1) * 128], rhs=wb[:, kt, :],
                          start=(kt == 0), stop=(kt == KT - 1), sig=False)
                tok0 = t0 + st * 128
                if which == 0:
                    s = stgb[si % 2]
                    fw.copy(s[:, :], ps[:, :512], eng=("act" if si % 2 else "dve"))
                    fw.dma(dram_ap(P.dr["Pv"], tok0 * 512, [[512, 128], [1, 512]]), s[:, :], queue=("act" if si % 2 else "pool"))
                else:
                    s = stg[si % 4]
                    fw.copy(s[:, :], ps[:, :512], eng=("act" if si % 2 else "dve"))
                    fw.dma(dram_ap(P.dr["Psv"], tok0 * 512, [[512, 128], [1, 512]]), s[:, :], queue=("act" if si % 2 else "pool"))
                si += 1
        for st in range(n // 128):
            ps = fw.psum()
            for kt in range(KT):
                fw.mm(ps[:, :16], lhsT=hT[:, kt, st * 128:(st + 1) * 128], rhs=wab[:, kt, :],
                      start=(kt == 0), stop=(kt == KT - 1))
            s = stg[si % 4]
            si += 1
            fw.copy(s[:, :16], ps[:, :16])
            tok0 = st * 128 + t0
            fw.dma(dram_ap(P.dr["Pab"], tok0 * 16, [[16, 128], [1, 16]]), s[:, 0:16], queue="pool")
    fw.pop_scope()


def phase_c(P, l):
    c, fw = P.cfg, P.fw
    D, KT, Th, L, TL, FT = c.D, c.KT, c.Th, c.L, c.TL, c.FT
    last = (l == c.NL - 1)
    sel = last and P.split
    if sel:
        Th2 = P.Th2
        fw.barrier()
        for nm, rows in (("xmy", D), ("hT", D), ("yT", 2048)):
            fw.dma(dram_ap(P.dr[nm + "_sel"], 0, [[Th2, 128], [128 * Th2, rows // 128], [1, Th2]]),
                   dram_ap(P.dr[nm], P.reg_half, [[TL, 128], [128 * TL, rows // 128], [1, Th2]]), queue="pool")
        fw.barrier()
        blocks = [(t, min(512, Th2 - t), False) for t in range(0, Th2, 512)]
        xsrc, hsrc, ysrc, rs = P.dr["xmy_sel"], P.dr["hT_sel"], P.dr["yT_sel"], Th2
    else:
        blocks = c.blocks
        xsrc, hsrc, ysrc, rs = P.dr["xmy"], P.dr["hT"], P.dr["yT"], TL
    fw.push_scope()
    xb = fw.sbuf("xbC", [128, KT, 512], F32)
    wbs = [fw.sbuf("wbC", [128, KT, 512], BF16) for _ in range(3)]
    wbr = [fw.sbuf("wbrC", [128, 4, 512], BF16) for _ in range(2)]
    stg = [fw.sbuf("stgC", [128, 512], F32) for _ in range(3)]
    pools = {"sq": [fw.sbuf("sqC", [128, 512], BF16) for _ in range(2)],
             "rt": fw.sbuf("rtC", [128, 512], F32), "rs": fw.sbuf("rsC", [128, 512], F32),
             "tmp": [fw.sbuf("tmpC", [128, 512], F32) for _ in range(2)]}
    wi = 0
    si = 0
    for bi, (t0, n, is_ctx) in enumerate(blocks):
        if is_ctx and last:
            continue
        r = 1 if is_ctx else 0
        if l == 0:
            src = (dram_ap(P.inp["ctxT"], 0, [[L, 128], [128 * L, KT], [1, n]]) if is_ctx else
                   dram_ap(P.inp["xh"], t0, [[Th, 128], [128 * Th, KT], [1, n]]))
        else:
            src = dram_ap(xsrc, t0, [[rs, 128], [128 * rs, KT], [1, n]])
        fw.dma(xb[:, :, :n], src, queue="act")
        fw.push_scope()
        hT = fw.sbuf("hTC", [128, KT, 512], BF16)
        yT = fw.sbuf("yTC", [128, 16, 512], BF16)
        mg = fw.sbuf("mg", [128, KT, 512], F32)
        mgb = fw.sbuf("mgb", [128, KT, 512], BF16)
        fw.dma(hT[:, :, :n], dram_ap(hsrc, t0, [[rs, 128], [128 * rs, KT], [1, n]]), queue="act")
        fw.dma(yT[:, :, :n], dram_ap(ysrc, t0, [[rs, 128], [128 * rs, 16], [1, n]]), queue="pool")
        for i in range(4):
            for mb in range(D // 512):
                wb = wbs[wi % 3]
                wr_ = wbr[wi % 2]
                wi += 1
                fw.dma(wb[:, :, :], wview(P, "w_gate%d_%d" % (l, i), 0, KT, mb * 512, 512))
                fw.dma(wr_[:, :, :], wview(P, "w_br%d_%d" % (l, i), 0, 4, mb * 512, 512))
                for mi in range(4):
                    m = mb * 4 + mi
                    psg = fw.psum()
                    for kt in range(KT):
                        fw.mm(psg[:, :n], lhsT=wb[:, kt, mi * 128:(mi + 1) * 128], rhs=hT[:, kt, :n],
                              start=(kt == 0), stop=(kt == KT - 1), sig=False)
                    psb = fw.psum()
                    for k4 in range(4):
                        fw.mm(psb[:, :n], lhsT=wr_[:, k4, mi * 128:(mi + 1) * 128], rhs=yT[:, i * 4 + k4, :n],
                              start=(k4 == 0), stop=(k4 == 3))
                    sg = stg[si % 3]
                    si += 1
                    o = (l * 4 + i) * KT + m
                    fw.act(sg[:, :n], psg[:, :n], AF.Sigmoid, bias=P.bgate[:, o:o + 1])
                    if i == 0:
                        fw.tt(mg[:, m, :n], sg[:, :n], psb[:, :n], ALU.mult, append=(m > 0))
                    else:
                        fw.tt(sg[:, :n], sg[:, :n], psb[:, :n], ALU.mult)
                        fw.tt(mg[:, m, :n], mg[:, m, :n], sg[:, :n], ALU.add, eng="pool", append=True)
        for kt in range(KT):
            fw.copy(mgb[:, kt, :n], mg[:, kt, :n], eng=("act" if kt % 2 else "pool"), append=(kt > 0))
        for mb in range(D // 512):
            wb = wbs[wi % 3]
            wi += 1
            fw.dma(wb[:, :, :], wview(P, "w_out%d" % l, 0, KT, mb * 512, 512))
            for mi in range(4):
                m = mb * 4 + mi
                ps = fw.psum()
                for kt in range(KT):
                    fw.mm(ps[:, :n], lhsT=wb[:, kt, mi * 128:(mi + 1) * 128], rhs=mgb[:, kt, :n],
                          start=(kt == 0), stop=(kt == KT - 1), sig=False)
                fw.stt(xb[:, m, :n], ps[:, :n], modp(P, l, r, 2, m), xb[:, m, :n], op0=ALU.mult, op1=ALU.add,
                       append=True)
        fw.pop_scope()
        fw.push_scope()
        h2 = fw.sbuf("h2", [128, KT, 512], BF16)
        gT = fw.sbuf("gT", [128, FT, 512], BF16)
        emit_norm(P, xb, n, l, r, 1, h2, pools)
        for fb in range(FT // 4):
            w1 = wbs[wi % 3]
            wi += 1
            w3 = wbs[wi % 3]
            wi += 1
            fw.dma(w1[:, :, :], wview(P, "w1_%d" % l, 0, KT, fb * 512, 512))
            fw.dma(w3[:, :, :], wview(P, "w3_%d" % l, 0, KT, fb * 512, 512))
            for fi in range(4):
                f = fb * 4 + fi
                p1 = fw.psum()
                for kt in range(KT):
                    fw.mm(p1[:, :n], lhsT=w1[:, kt, fi * 128:(fi + 1) * 128], rhs=h2[:, kt, :n],
                          start=(kt == 0), stop=(kt == KT - 1), sig=False)
                p3 = fw.psum()
                for kt in range(KT):
                    fw.mm(p3[:, :n], lhsT=w3[:, kt, fi * 128:(fi + 1) * 128], rhs=h2[:, kt, :n],
                          start=(kt == 0), stop=(kt == KT - 1), sig=False)
                sl = stg[si % 3]
                si += 1
                fw.act(sl[:, :n], p1[:, :n], AF.Silu)
                fw.tt(gT[:, f, :n], sl[:, :n], p3[:, :n], ALU.mult, append=(f > 0))
        KC = FT // 4
        for mb in range(D // 512):
            pss = [fw.psum() for _ in range(4)]
            for kc in range(4):
                wb = wbs[wi % 3]
                wi += 1
                fw.dma(wb[:, :KC, :], wview(P, "w2_%d" % l, kc * KC, KC, mb * 512, 512))
                for mi in range(4):
                    for kk in range(KC):
                        fw.mm(pss[mi][:, :n], lhsT=wb[:, kk, mi * 128:(mi + 1) * 128], rhs=gT[:, kc * KC + kk, :n],
                              start=(kc == 0 and kk == 0), stop=(kc == 3 and kk == KC - 1), sig=(kk == KC - 1))
            for mi in range(4):
                m = mb * 4 + mi
                fw.stt(xb[:, m, :n], pss[mi][:, :n], modp(P, l, r, 5, m), xb[:, m, :n], op0=ALU.mult, op1=ALU.add,
                       append=True)
        if last:
            dst = dram_ap(P.out, t0, [[P.Th2, 128], [128 * P.Th2, KT], [1, n]])
        else:
            dst = dram_ap(P.dr["xmy"], t0, [[TL, 128], [128 * TL, KT], [1, n]])
        fw.dma(dst, xb[:, :, :n], queue="act")
        fw.pop_scope()
    fw.pop_scope()


def mixer_sgu(P, l, need_ctx):
    for _ in sgu_gen(P, l, need_ctx):
        pass


def sgu_gen(P, l, need_ctx):
    c, fw = P.cfg, P.fw
    Th, L, TL = c.Th, c.L, c.TL
    fw.push_scope()
    w32 = fw.sbuf("sgw32", [128, 512], F32)
    fw.dma(w32[:, :], dram_ap(P.inp["sg_wT"], l * 512, [[c.NL * 512, 128], [1, 512]]))
    wTb = fw.sbuf("sgwb", [128, 512], BF16)
    fw.copy(wTb[:, :], w32[:, :])
    b32 = fw.sbuf("sgb32", [1, 512], F32)
    fw.dma(b32[:, :], dram_ap(P.inp["sg_b"], l * 512, [[c.NL * 512, 1], [1, 512]]))
    bb = fw.sbuf("sgbb", [1, 512], BF16)
    fw.copy(bb[:, :], b32[:, :])
    vnbc = fw.sbuf("vnbc", [128, 512], F32)
    fw.dma(vnbc[:, :], dram_ap(P.inp["sg_vn"], l * 512, [[0, 128], [1, 512]]))
    vb_ = [fw.sbuf("sgv", [128, 512], F32) for _ in range(2)]
    gb_ = [fw.sbuf("sgg", [128, 512], F32) for _ in range(2)]
    vnb_ = [fw.sbuf("sgvn", [128, 512], BF16) for _ in range(2)]
    ub_ = [fw.sbuf("sgu", [128, 4, 128], F32) for _ in range(2)]
    ug_ = [fw.sbuf("sgug", [128, 4, 128], F32) for _ in range(2)]
    ys_ = [fw.sbuf("sgy", [128, 4, 128], BF16) for _ in range(2)]
    st_ = [fw.sbuf("sgst", [128, 6], F32) for _ in range(2)]
    mv_ = [fw.sbuf("sgmv", [128, 2], F32) for _ in range(2)]
    rs_ = [fw.sbuf("sgrs", [128, 2], F32) for _ in range(2)]
    ntile = c.NT + (L // 128 if need_ctx else 0)
    for i in range(ntile):
        tok0 = i * 128
        v, g, vnb, u, ug, ys, st, mv, rs = (x[i % 2] for x in (vb_, gb_, vnb_, ub_, ug_, ys_, st_, mv_, rs_))
        fw.dma(v[:, :], dram_ap(P.dr["Psv"], tok0 * 512, [[512, 128], [1, 512]]))
        fw.dma(u[:, :, :], dram_ap(P.dr["Pu"], tok0, [[TL, 128], [128 * TL, 4], [1, 128]]), queue="pool")
        fw.act(g[:, :], v[:, :], AF.Gelu_apprx_tanh)
        fw.op("dve", lambda E: E.bn_stats(out=st.t[:, :], in_=g.t[:, :]), reads=[g[:, :]], writes=[st[:, :]])
        fw.op("dve", lambda E: E.bn_aggr(out=mv.t[:, :], in_=st.t[:, :]), reads=[st[:, :]], writes=[mv[:, :]])
        fw.act(rs[:, 0:1], mv[:, 1:2], AF.Sqrt, bias=1e-6)
        fw.recip(rs[:, 1:2], rs[:, 0:1])
        fw.ts(g[:, :], g[:, :], mv[:, 0:1], rs[:, 1:2], op0=ALU.subtract, op1=ALU.mult)
        fw.tt(vnb[:, :], g[:, :], vnbc[:, :], ALU.mult, eng="pool")
        fw.act(ug[:, :, :], u[:, :, :], AF.Gelu_apprx_tanh)
        for gi in range(4):
            ps = fw.psum()
            fw.mm(ps[:, :128], lhsT=vnb[:, gi * 128:(gi + 1) * 128], rhs=wTb[:, gi * 128:(gi + 1) * 128],
                  start=True, stop=False)
            fw.mm(ps[:, :128], lhsT=P.ones_b[0:1, :], rhs=bb[0:1, gi * 128:(gi + 1) * 128], start=False, stop=True)
            fw.tt(ys[:, gi, :], ug[:, gi, :], ps[:, :128], ALU.mult, append=(gi > 0))
        fw.dma(dram_ap(P.dr["yT"], 1536 * TL + tok0, [[TL, 128], [128 * TL, 4], [1, 128]]), ys[:, :, :])
        yield
    fw.pop_scope()


def fm_rmsnorm(P, dst, src, n, gain, scale_mean, post_scale, pools, eps=1e-6):
    fw = P.fw
    i = pools["i"]
    pools["i"] += 1
    sq = pools["sq"][i % 2]
    rt = pools["rt"][i % 2]
    fw.act(sq[:, :n], src, AF.Square)
    ps = fw.psum()
    fw.mm(ps[:, :n], lhsT=P.ones_f[:, :], rhs=sq[:, :n])
    k = 1.0 / (post_scale * post_scale)
    fw.act(rt[:, :n], ps[:, :n], AF.Sqrt, bias=eps * k, scale=scale_mean * k)
    fw.recip(rt[:, :n], rt[:, :n])
    if gain is None:
        fw.tt(dst, src, rt[:, :n], ALU.mult)
    else:
        fw.stt(dst, src, gain, rt[:, :n], op0=ALU.mult, op1=ALU.mult)


def mixer_na(P, l, need_ctx, co=None):
    c, fw = P.cfg, P.fw
    T, L, TL, NT = c.T, c.L, c.TL, c.NT
    NP = P.NPAT
    NC = L // 128
    scale = 128 ** -0.5
    fw.push_scope()
    pools = {"i": 0, "sq": [fw.sbuf("nsq", [128, 512], F32) for _ in range(2)],
             "rt": [fw.sbuf("nrt", [128, 512], F32) for _ in range(2)]}
    qraw = fw.sbuf("qraw", [128, TL], F32)
    kraw = fw.sbuf("kraw", [128, TL], F32)
    qn = fw.sbuf("qn", [128, TL], BF16)
    kn = fw.sbuf("kn", [128, TL], BF16)
    vall = fw.sbuf("vall", [128, TL // 128, 128], BF16)
    eraw = fw.sbuf("eraw", [128, NP, 128], F32)
    E = fw.sbuf("E", [128, NP, 128], BF16)
    yst = fw.sbuf("nay", [128, TL], BF16)
    pa_ = [fw.sbuf("pa", [128, 512], BF16) for _ in range(3)]
    pb_ = [fw.sbuf("pb", [128, 512], BF16) for _ in range(3)]
    rd_ = [fw.sbuf("rd", [128, 128], F32) for _ in range(2)]
    for h in range(4):
        fw.dma(qraw[:, :], fm_rows(P.dr["Pq"], h * 128, TL, 0, TL))
        fw.dma(kraw[:, :], fm_rows(P.dr["Pk"], h * 128, TL, 0, TL), queue="pool")
        fw.dma(vall[:, :, :], dram_ap(P.dr["Pv"], h * 128, [[512, 128], [128 * 512, TL // 128], [1, 128]]), queue="pool")
        fw.dma(eraw[:, :, :], dram_ap(P.inp["na_bias"], ((l * 4 + h) * NP) * 128 * 128, [[128, 128], [128 * 128, NP], [1, 128]]))
        for q0 in range(0, NP, 4):
            q1 = min(NP, q0 + 4)
            fw.act(E[:, q0:q1, :], eraw[:, q0:q1, :], AF.Exp, append=(q0 > 0))
        for t0 in range(0, TL, 512):
            n = min(512, TL - t0)
            fm_rmsnorm(P, qn[:, t0:t0 + n], qraw[:, t0:t0 + n], n, P.hnorm[:, l * 3 + 0:l * 3 + 1], 1.0 / 128, 1.0, pools)
            fm_rmsnorm(P, kn[:, t0:t0 + n], kraw[:, t0:t0 + n], n, P.hnorm[:, l * 3 + 1:l * 3 + 2], 1.0 / 128, 1.0, pools)
        nq = NT + (NC if need_ctx else 0)
        ctx_slots = [(NT + s_, None) for s_ in range(NC)]
        def reg(buf, si):
            return (buf[0] if si < 4 else buf[1])[:, (si % 4) * 128:(si % 4 + 1) * 128]

        def stage1(jj):
            is_c = jj >= NT
            qv = qn[:, jj * 128:(jj + 1) * 128]
            if is_c:
                slots = ctx_slots
            else:
                cls = na_class(jj, NT)
                slots = [(jj + dm, cls * 7 + dm + 3) for dm in range(-3, 4)
                         if 0 <= jj + dm < NT and na_needed(jj, dm, NT)] + ctx_slots
            pa = pa_[jj % 3]
            pb = pb_[jj % 3]
            psa = fw.psum()
            psb = fw.psum()
            for si, (sl, pat) in enumerate(slots):
                fw.mm(reg((psa, psb), si), lhsT=kn[:, sl * 128:(sl + 1) * 128], rhs=qv, start=True, stop=True)
            na = min(4, len(slots))
            fw.act(pa[:, :na * 128], psa[:, :na * 128], AF.Exp, scale=scale)
            if len(slots) > 4:
                nbk = len(slots) - 4
                fw.act(pb[:, :nbk * 128], psb[:, :nbk * 128], AF.Exp, scale=scale)
            for si, (sl, pat) in enumerate(slots):
                if pat is not None:
                    tgt = reg((pa, pb), si)
                    fw.tt(tgt, tgt, E[:, pat, :], ALU.mult, eng=("dve" if si % 2 else "pool"))
            return jj, slots, pa, pb

        def stage2(ctx_):
            jj, slots, pa, pb = ctx_
            pso = fw.psum()
            psd = fw.psum()
            for si, (sl, pat) in enumerate(slots):
                pv = reg((pa, pb), si)
                fw.mm(pso[:, :128], lhsT=vall[:, sl, :], rhs=pv, start=(si == 0), stop=(si == len(slots) - 1))
                fw.mm(psd[:, :128], lhsT=P.ones_b[:, :], rhs=pv, start=(si == 0), stop=(si == len(slots) - 1))
            rd = rd_[jj % 2]
            fw.recip(rd[:, :], psd[:, :128])
            fw.tt(yst[:, jj * 128:(jj + 1) * 128], pso[:, :128], rd[:, :], ALU.mult, append=(jj > 0))

        prev = None
        for jj in range(nq):
            cur = stage1(jj)
            if prev is not None:
                stage2(prev)
            prev = cur
            if co is not None and jj % 4 == 3:
                next(co, None)
        stage2(prev)
        fw.dma(fm_rows(P.dr["yT"], h * 128, TL, 0, nq * 128), yst[:, :nq * 128])
    fw.pop_scope()


def na_class(jj, NT):
    if jj < 2:
        return jj
    if jj >= NT - 2:
        return 3 + (jj - (NT - 2))
    return 2


def na_needed(jj, dm, NT):
    rows = 2 * NT
    kr_n = min(8, rows)
    for b in range(2):
        qr = 2 * jj + b
        s0 = min(max(qr - kr_n // 2, 0), rows - kr_n)
        for a in range(2):
            kr = 2 * (jj + dm) + a
            if s0 <= kr < s0 + kr_n:
                return True
    return False


def pd_row(typ, h):
    pr, hh = h // 2, h % 2
    return pr * PAIR_ROWS + typ * 256 + hh * 128


def mixer_ft(P, l, need_ctx):
    c, fw = P.cfg, P.fw
    L, T, NT, TL = c.L, c.T, c.NT, c.TL
    NTT = TL // 128
    fw.push_scope()
    cd32 = fw.sbuf("cd32", [128, 256], F32)
    fw.dma(cd32[:, :], P.inp["cdsd"].ap()[:, :])
    cdb = fw.sbuf("cdb", [128, 256], BF16)
    fw.copy(cdb[:, :], cd32[:, :])
    x32 = [fw.sbuf("ftx32", [128, 2048], F32) for _ in range(2)]
    xb = [fw.sbuf("ftxb", [128, TL], BF16) for _ in range(2)]
    AB = [fw.sbuf("ftAB", [128, NTT, 256], BF16) for _ in range(4)]
    ci = 0
    for g in range(4):
        row0 = pd_row(4, g)
        for s0 in range(0, TL, 2048):
            n = min(2048, TL - s0)
            xx = x32[ci % 2]
            ci += 1
            fw.dma(xx[:, :n], fm_rows(P.dr["PD"], row0, TL, s0, n), queue=("pool" if ci % 2 else "sp"))
            fw.copy(xb[g % 2][:, s0:s0 + n], xx[:, :n], eng=("pool" if ci % 2 else "dve"), append=(s0 > 0))
        for tt in range(NTT):
            ps = fw.psum()
            fw.mm(ps[:, :256], lhsT=xb[g % 2][:, tt * 128:(tt + 1) * 128], rhs=cdb[:, :])
            fw.copy(AB[g][:, tt, :], ps[:, :256], eng=("act" if tt % 2 else "dve"), append=(tt > 0))
    CB = 256
    cblk = [fw.sbuf("ftc", [128, NT, CB], BF16) for _ in range(2)]
    sblk = [fw.sbuf("fts", [128, NT, CB], BF16) for _ in range(2)]
    ysg = [fw.sbuf("fty", [128, CB], BF16) for _ in range(3)]
    yi = 0
    for cb in range(T // CB):
        cbf = cblk[cb % 2]
        sbf = sblk[cb % 2]
        fw.dma(cbf[:, :, :], dram_ap(P.W["ct"], cb * CB, [[T, 128], [128 * T, NT], [1, CB]]))
        fw.dma(sbf[:, :, :], dram_ap(P.W["st"], cb * CB, [[T, 128], [128 * T, NT], [1, CB]]), queue="pool")
        for g in range(4):
            ps = fw.psum()
            for tt in range(NT):
                fw.mm(ps[:, :CB], lhsT=AB[g][:, tt, 0:128], rhs=cbf[:, tt, :], start=(tt == 0), stop=False, sig=False)
                fw.mm(ps[:, :CB], lhsT=AB[g][:, tt, 128:256], rhs=sbf[:, tt, :], start=False, stop=(tt == NT - 1),
                      sig=False)
            ys = ysg[yi % 3]
            yi += 1
            fw.copy(ys[:, :], ps[:, :CB], eng=("act" if yi % 2 else "dve"))
            fw.dma(fm_rows(P.dr["yT"], 1024 + g * 128, TL, cb * CB, CB), ys[:, :])
    if need_ctx:
        nt = L // 128
        cc = fw.sbuf("ftcc", [128, nt, 2 * L], BF16)
        fw.dma(cc[:, :, :], dram_ap(P.inp["ctx_cs"], 0, [[2 * L, 128], [128 * 2 * L, nt], [1, 2 * L]]))
        for g in range(4):
            ps = fw.psum()
            for tt in range(nt):
                fw.mm(ps[:, :L], lhsT=AB[g][:, NT + tt, 0:128], rhs=cc[:, tt, 0:L], start=(tt == 0), stop=False)
                fw.mm(ps[:, :L], lhsT=AB[g][:, NT + tt, 128:256], rhs=cc[:, tt, L:2 * L], start=False,
                      stop=(tt == nt - 1))
            ys = ysg[yi % 3]
            yi += 1
            fw.copy(ys[:, :L], ps[:, :L])
            fw.dma(fm_rows(P.dr["yT"], 1024 + g * 128, TL, T, L), ys[:, :L])
    fw.pop_scope()


def bc_mid(v, n):
    a = v.ap
    pat = [list(x) for x in a.ap]
    return V(v.buf, bass.AP(a.tensor, a.offset, [pat[0], [0, n], pat[1]]))


def mixer_dn(P, l, need_ctx):
    c, fw = P.cfg, P.fw
    Th, L, T = c.Th, c.L, c.T
    W = T + L
    NTT = W // 128
    NC = L // 128
    TL = c.TL
    fw.push_scope()
    pools = {"i": 0, "sq": [fw.sbuf("dsq", [128, 512], F32) for _ in range(2)],
             "rt": [fw.sbuf("drt", [128, 512], F32) for _ in range(2)]}
    raws = [fw.sbuf("draw", [128, W], F32) for _ in range(2)]
    cvs = [fw.sbuf("dcv", [128, W], F32) for _ in range(2)]
    tst = [fw.sbuf("dtst", [128, 128], F32) for _ in range(3)]
    cw = fw.sbuf("dcw", [128, c.NL * 60], F32)
    fw.dma(cw[:, :], P.inp["dn_conv"].ap()[:, :])
    k_i = 0
    ti = 0
    for hh in range(4):
        for typ in range(3):
            raw = raws[k_i % 2]
            cv = cvs[k_i % 2]
            k_i += 1
            row0 = pd_row(typ, hh)
            fw.dma(raw[:, 0:L], fm_rows(P.dr["PD"], row0, TL, T, L))
            fw.dma(raw[:, L:], fm_rows(P.dr["PD"], row0, TL, 0, T), append=True, queue="pool")
            wo = ((l * 3 + typ) * 4 + hh) * 5
            for (s0, sl) in ((0, L), (L, T)):
                fw.ts(cv[:, s0:s0 + sl], raw[:, s0:s0 + sl], cw[:, wo + 2:wo + 3], None, op0=ALU.mult,
                      append=(s0 > 0))
                for j in (0, 1, 3, 4):
                    s = j - 2
                    lo = max(0, -s)
                    hi = sl - max(0, s)
                    eng = "dve" if j in (0, 3) else "pool"
                    fw.stt(cv[:, s0 + lo:s0 + hi], raw[:, s0 + lo + s:s0 + hi + s], cw[:, wo + j:wo + j + 1],
                           cv[:, s0 + lo:s0 + hi], op0=ALU.mult, op1=ALU.add, eng=eng, append=True)
            for t0 in range(0, W, 2048):
                n = min(2048, W - t0)
                fw.act(cv[:, t0:t0 + n], cv[:, t0:t0 + n], AF.Silu, append=True)
            if typ < 2:
                post = (128 ** -0.5) if typ == 0 else 1.0
                for t0 in range(0, W, 512):
                    n = min(512, W - t0)
                    fm_rmsnorm(P, cv[:, t0:t0 + n], cv[:, t0:t0 + n], n, None, 1.0, post, pools)
                name = "DNq" if typ == 0 else "DNk"
                fw.dma(dram_ap(P.dr[name], hh * 128 * W, [[W, 128], [1, W]]), cv[:, :])
            if typ >= 1:
                name = "DNkt" if typ == 1 else "DNvt"
                for tt in range(NTT):
                    ps = fw.psum()
                    fw.mm(ps[:, :128], lhsT=cv[:, tt * 128:(tt + 1) * 128], rhs=P.ident)
                    s = tst[ti % 3]
                    ti += 1
                    fw.copy(s[:, :], ps[:, :128], eng=("act" if ti % 2 else "dve"))
                    fw.dma(dram_ap(P.dr[name], (hh * W + tt * 128) * 128, [[128, 128], [1, 128]]), s[:, :],
                           queue=("pool" if ti % 2 else "sp"))
    fw.pop_scope()
    fw.push_scope()
    ab = fw.sbuf("dab", [128, NTT, 16], F32)
    fw.dma(ab[:, 0:NC, :], dram_ap(P.dr["Pab"], T * 16, [[16, 128], [128 * 16, NC], [1, 16]]))
    fw.dma(ab[:, NC:, :], dram_ap(P.dr["Pab"], 0, [[16, 128], [128 * 16, NTT - NC], [1, 16]]), append=True)
    par = fw.sbuf("dpar", [128, 16], F32)
    fw.dma(par[:, :], dram_ap(P.inp["dn_par"], l * 16, [[0, 128], [1, 16]]))
    nA = fw.sbuf("dnA", [128, 8], F32)
    fw.act(nA[:, :], par[:, 0:8], AF.Exp)
    fw.ts(nA[:, :], nA[:, :], -1.0, None, op0=ALU.mult)
    g = fw.sbuf("dg", [128, NTT, 8], F32)
    beta = fw.sbuf("dbeta", [128, NTT, 8], F32)
    fw.tt(g[:, :, :], ab[:, :, 0:8], bc_mid(par[:, 8:16], NTT), ALU.add)
    fw.act(g[:, :, :], g[:, :, :], AF.Exp)
    fw.act(g[:, :, :], g[:, :, :], AF.Ln, bias=1.0)
    fw.tt(g[:, :, :], g[:, :, :], bc_mid(nA[:, :], NTT), ALU.mult)
    fw.act(beta[:, :, :], ab[:, :, 8:16], AF.Sigmoid)
    gcl = fw.sbuf("dgcl", [128, NTT, 16], F32)
    for tt in range(NTT):
        ps = fw.psum()
        fw.mm(ps[:, 0:4], lhsT=P.Mf, rhs=g[:, tt, 0:4])
        fw.mm(ps[:, 4:8], lhsT=P.Mb, rhs=g[:, tt, 4:8])
        fw.mm(ps[:, 8:16], lhsT=P.Mch, rhs=g[:, tt, 0:8])
        fw.copy(gcl[:, tt, :], ps[:, 0:16], append=(tt > 0))
    egc = fw.sbuf("degc", [128, NTT, 8], F32)
    fw.act(egc[:, :, :], gcl[:, :, 0:8], AF.Exp)
    bg = fw.sbuf("dbg", [128, NTT, 8], F32)
    fw.tt(bg[:, :, :], beta[:, :, :], egc[:, :, :], ALU.mult)
    kdec = fw.sbuf("dkdec", [128, NTT, 8], F32)
    fw.tt(kdec[:, :, :], gcl[:, :, 8:16], gcl[:, :, 0:8], ALU.subtract)
    fw.act(kdec[:, :, :], kdec[:, :, :], AF.Exp)
    if getattr(c, "debug", ()):
        for nm, bf, k_ in (("g", g, 8), ("beta", beta, 8), ("gcl", gcl, 16), ("kdec", kdec, 8), ("bg", bg, 8)):
            o = P.nc.dram_tensor("dbg_" + nm, [128, NTT * k_], F32, kind="ExternalOutput")
            fw.dma(o.ap()[:, :], bf[:, :, :].rearrange("p a b -> p (a b)"))
    NBL = 8
    NBS = 10
    long_names = ["qT", "kT", "ktm", "vtm", "DmT", "EGB", "u0", "u1", "kd", "qkT", "qgT", "egl", "vnew", "oT"]
    pool_ = {nm: [fw.sbuf("m_" + nm, [128, 128], F32) for _ in range(NBL)] for nm in long_names}
    for nm in ("X", "Y", "P"):
        pool_[nm] = [fw.sbuf("m_" + nm, [128, 128], F32) for _ in range(NBS)]
    pool_["tmp"] = [fw.sbuf("m_tmp", [128, 128], F32) for _ in range(16)]
    cnt = {nm: 0 for nm in pool_}
    wT0 = [fw.sbuf("wT0", [128, 128], F32) for _ in range(NBL)]
    wT1 = [fw.sbuf("wT1", [128, 128], F32) for _ in range(NBL)]
    for b_ in wT0 + wT1:
        fw.memset(b_[:, :], 0.0)
    wcnt = [0]

    def nb(nm):
        b_ = pool_[nm][cnt[nm] % len(pool_[nm])]
        cnt[nm] += 1
        return b_

    S = {}
    for d in range(2):
        for hh in range(4):
            S[(d, hh)] = [fw.sbuf("S%d%d" % (d, hh), [128, 128], F32) for _ in range(2)]
            fw.memset(S[(d, hh)][0][:, :], 0.0)
    scnt = {k: 0 for k in S}
    order = {0: list(range(NTT)), 1: list(range(NC - 1, -1, -1)) + list(range(NTT - 1, NC - 1, -1))}
    qi = [0]

    def chain_tile(d, hh, tt):
        col = d * 4 + hh
        need_o = need_ctx or tt >= NC
        gc_col = gcl[:, tt, col:col + 1]
        kT, ktm, vtm = nb("kT"), nb("ktm"), nb("vtm")
        q_ = "sp" if qi[0] % 2 else "pool"
        qi[0] += 1
        fw.dma(kT[:, :], dram_ap(P.dr["DNk"], hh * 128 * W + tt * 128, [[W, 128], [1, 128]]), queue=q_)
        fw.dma(ktm[:, :], dram_ap(P.dr["DNkt"], (hh * W + tt * 128) * 128, [[128, 128], [1, 128]]), queue=q_)
        fw.dma(vtm[:, :], dram_ap(P.dr["DNvt"], (hh * W + tt * 128) * 128, [[128, 128], [1, 128]]), queue=q_)
        if need_o:
            qT = nb("qT")
            fw.dma(qT[:, :], dram_ap(P.dr["DNq"], hh * 128 * W + tt * 128, [[W, 128], [1, 128]]), queue=q_)
        yield
        psG = fw.psum()
        fw.mm(psG[:, :128], lhsT=kT[:, :], rhs=kT[:, :])
        diag = nb("tmp")
        fw.act(diag[:, :], P.ident, AF.Copy, scale=gc_col)
        psB = fw.psum()
        fw.mm(psB[:, :128], lhsT=P.ones_f[:, :], rhs=diag[:, :])
        mx = nb("tmp")
        fw.ts(mx[:, :], psB[:, :128], gc_col, 0.0, op0=ALU.subtract, op1=ALU.max)
        Dm = nb("tmp")
        fw.act(Dm[:, :], mx[:, :], AF.Exp, scale=-1.0)
        if need_o:
            m2 = nb("tmp")
            fw.ts(m2[:, :], psB[:, :128], gc_col, 0.0, op0=ALU.subtract, op1=ALU.min)
            DmT = nb("DmT")
            fw.act(DmT[:, :], m2[:, :], AF.Exp)
            EGB = nb("EGB")
            fw.act(EGB[:, :], psB[:, :128], AF.Exp)
        t1 = nb("tmp")
        fw.tt(t1[:, :], psG[:, :128], Dm[:, :], ALU.mult)
        A = nb("Y")
        fw.stt(A[:, :], t1[:, :], beta[:, tt, col:col + 1], P.ms[d], op0=ALU.mult, op1=ALU.mult)
        yield
        psU = fw.psum()
        fw.mm(psU[:, :128], lhsT=A[:, :], rhs=P.ident)
        U = nb("X")
        fw.copy(U[:, :], psU[:, :128], eng="act")
        Pm = nb("P")
        fw.tt(Pm[:, :], P.ident, U[:, :], ALU.subtract, eng="pool")
        yield
        X, Y = U, A
        for k in range(1, 6):
            psY = fw.psum()
            fw.mm(psY[:, :128], lhsT=X[:, :], rhs=Y[:, :])
            Y2 = nb("Y")
            fw.copy(Y2[:, :], psY[:, :128], eng="act")
            if k < 5:
                psX = fw.psum()
                fw.mm(psX[:, :128], lhsT=Y[:, :], rhs=X[:, :])
                X2 = nb("X")
                fw.copy(X2[:, :], psX[:, :128], eng="dve")
            yield
            psP = fw.psum()
            fw.mm(psP[:, :128], lhsT=Y2[:, :], rhs=Pm[:, :])
            P2 = nb("P")
            fw.tt(P2[:, :], Pm[:, :], psP[:, :128], ALU.add)
            Pm = P2
            Y = Y2
            if k < 5:
                X = X2
            yield
        TT_ = Pm
        vb, kbg, kd = nb("tmp"), nb("tmp"), nb("kd")
        fw.act(vb[:, :], vtm[:, :], AF.Copy, scale=beta[:, tt, col:col + 1])
        fw.act(kbg[:, :], ktm[:, :], AF.Copy, scale=bg[:, tt, col:col + 1])
        fw.ts(kd[:, :], ktm[:, :], kdec[:, tt, col:col + 1], None, op0=ALU.mult, eng="pool")
        psu = fw.psum()
        fw.mm(psu[:, :128], lhsT=TT_[:, :], rhs=vb[:, :])
        u0, u1 = nb("u0"), nb("u1")
        fw.ts(u0[:, :], psu[:, :128], P.ind2[:, 0:1], None, op0=ALU.mult)
        fw.ts(u1[:, :], psu[:, :128], P.ind2[:, 1:2], None, op0=ALU.mult)
        psw = fw.psum()
        fw.mm(psw[:, :128], lhsT=kbg[:, :], rhs=TT_[:, :])
        w0 = wT0[wcnt[0] % NBL]
        w1 = wT1[wcnt[0] % NBL]
        wcnt[0] += 1
        fw.copy(w0[:, 0:64], psw[:, 0:64], eng="act")
        fw.copy(w1[:, 64:128], psw[:, 64:128], eng="act")
        yield
        if need_o:
            psQ = fw.psum()
            fw.mm(psQ[:, :128], lhsT=kT[:, :], rhs=qT[:, :])
            qkT = nb("qkT")
            fw.tt(qkT[:, :], psQ[:, :128], DmT[:, :], ALU.mult)
            fw.tt(qkT[:, :], qkT[:, :], P.miT[d], ALU.mult, eng="pool")
            qgT = nb("qgT")
            fw.tt(qgT[:, :], qT[:, :], EGB[:, :], ALU.mult, eng="pool")
        gsel = nb("tmp")
        fw.ts(gsel[:, 0:2], P.ind2, g[:, tt, col:col + 1], None, op0=ALU.mult, eng="pool")
        psE = fw.psum()
        fw.mm(psE[:, 0:2], lhsT=P.ones_f[:, :], rhs=gsel[:, 0:2])
        egl = nb("egl")
        fw.act(egl[:, 0:2], psE[:, 0:2], AF.Exp)
        if need_o:
            oT = nb("oT")
        yield
        chunks = (0, 1) if d == 0 else (1, 0)
        for ci in chunks:
            c0 = ci * 64
            Sc = S[(d, hh)][scnt[(d, hh)] % 2]
            Sn = S[(d, hh)][(scnt[(d, hh)] + 1) % 2]
            scnt[(d, hh)] += 1
            wc = w0 if ci == 0 else w1
            uc = u0 if ci == 0 else u1
            ps1 = fw.psum()
            fw.mm(ps1[:, :128], lhsT=wc[:, :], rhs=Sc[:, :])
            vnew = nb("vnew")
            fw.tt(vnew[:, :], uc[:, :], ps1[:, :128], ALU.subtract)
            yield
            if need_o:
                pso = fw.psum()
                fw.mm(pso[:, 0:64], lhsT=Sc[:, :], rhs=qgT[:, c0:c0 + 64], start=True, stop=False)
                fw.mm(pso[:, 0:64], lhsT=vnew[:, :], rhs=qkT[:, c0:c0 + 64], start=False, stop=True)
                fw.copy(oT[:, c0:c0 + 64], pso[:, 0:64], eng="act", append=(ci != chunks[0]))
            psS = fw.psum()
            fw.mm(psS[:, :128], lhsT=kd[:, :], rhs=vnew[:, :])
            fw.stt(Sn[:, :], Sc[:, :], egl[:, ci:ci + 1], psS[:, :128], op0=ALU.mult, op1=ALU.add)
            yield
        if need_o:
            fw.dma(dram_ap(P.dr["DNo"], ((d * 4 + hh) * 128) * W + tt * 128, [[W, 128], [1, 128]]), oT[:, :],
                   queue="sp")

    for step in range(NTT):
        for d in range(2):
            alive = [chain_tile(d, hh, order[d][step]) for hh in range(4)]
            while alive:
                nxt = []
                for g_ in alive:
                    try:
                        next(g_)
                        nxt.append(g_)
                    except StopIteration:
                        pass
                alive = nxt
    fw.pop_scope()
    fw.push_scope()
    pools = {"i": 0, "sq": [fw.sbuf("psq", [128, 512], F32) for _ in range(2)],
             "rt": [fw.sbuf("prt", [128, 512], F32) for _ in range(2)]}
    of_ = [fw.sbuf("pof", [128, 512], F32) for _ in range(2)]
    ob_ = [fw.sbuf("pob", [128, 512], F32) for _ in range(2)]
    z_ = [fw.sbuf("pz", [128, 512], F32) for _ in range(2)]
    y_ = [fw.sbuf("py", [128, 512], BF16) for _ in range(2)]
    i = 0
    for hh in range(4):
        zrow = pd_row(3, hh)
        segs = []
        if need_ctx:
            segs.append((0, L, T))
        for s0 in range(0, T, 512):
            segs.append((L + s0, min(512, T - s0), s0))
        for (w0_, n, tcol) in segs:
            of, ob, z, y = of_[i % 2], ob_[i % 2], z_[i % 2], y_[i % 2]
            i += 1
            fw.dma(of[:, :n], dram_ap(P.dr["DNo"], ((0 * 4 + hh) * 128) * W + w0_, [[W, 128], [1, n]]))
            fw.dma(ob[:, :n], dram_ap(P.dr["DNo"], ((1 * 4 + hh) * 128) * W + w0_, [[W, 128], [1, n]]), queue="pool")
            fw.dma(z[:, :n], fm_rows(P.dr["PD"], zrow, TL, tcol, n))
            fw.tt(of[:, :n], of[:, :n], ob[:, :n], ALU.add, eng="pool")
            fm_rmsnorm(P, ob[:, :n], of[:, :n], n, P.hnorm[:, l * 3 + 2:l * 3 + 3], 1.0 / 128, 1.0, pools)
            fw.act(z[:, :n], z[:, :n], AF.Silu)
            fw.tt(y[:, :n], ob[:, :n], z[:, :n], ALU.mult)
            fw.dma(fm_rows(P.dr["yT"], 512 + hh * 128, TL, tcol, n), y[:, :n])
    fw.pop_scope()


def zero_missing(P):
    c, fw = P.cfg, P.fw
    rows = {"na": 0, "dn": 512, "ft": 1024, "sgu": 1536}
    miss = [m for m in rows if m not in c.mixers]
    if not miss:
        return
    fw.push_scope()
    z = fw.sbuf("zz", [128, c.TL], BF16)
    fw.memset(z[:, :], 0.0)
    for m in miss:
        for k in range(4):
            fw.dma(fm_rows(P.dr["yT"], rows[m] + k * 128, c.TL, 0, c.TL), z[:, :])
    fw.pop_scope()


def build(cfg):
    P = Prog(cfg)
    c, fw = cfg, P.fw
    D, L, TL, T = c.D, c.L, c.TL, c.T
    W = T + L
    P.dram("xmy", [D, TL]); P.dram("hT", [D, TL], BF16)
    P.dram("Pq", [512, TL]); P.dram("Pk", [512, TL]); P.dram("Pu", [512, TL]); P.dram("PD", [2 * PAIR_ROWS, TL])
    P.dram("Pv", [TL, 512], BF16); P.dram("Psv", [TL, 512]); P.dram("Pab", [TL, 16])
    P.dram("yT", [2048, TL], BF16)
    if P.split:
        P.dram("xmy_sel", [D, P.Th2]); P.dram("hT_sel", [D, P.Th2], BF16); P.dram("yT_sel", [2048, P.Th2], BF16)
    P.dram("DNq", [512, W]); P.dram("DNk", [512, W]); P.dram("DNkt", [4 * W, 128]); P.dram("DNvt", [4 * W, 128])
    P.dram("DNo", [1024, W])
    stage = getattr(c, "stage", 99)
    load_consts(P)
    if stage >= 1:
        prep_weights(P)
    if stage >= 2:
        compute_mod(P)
    for l in range(c.NL if stage >= 3 else 0):
        need_ctx = l < c.NL - 1
        phase_a(P, l)
        fw.barrier()
        if stage == 3:
            break
        if "sgu" in c.mixers and "na" in c.mixers:
            co = sgu_gen(P, l, need_ctx)
            next(co)
            mixer_na(P, l, need_ctx, co)
            for _ in co:
                pass
        else:
            if "sgu" in c.mixers:
                mixer_sgu(P, l, need_ctx)
            if "na" in c.mixers:
                mixer_na(P, l, need_ctx)
        if "ft" in c.mixers:
            mixer_ft(P, l, need_ctx)
        if "dn" in c.mixers:
            mixer_dn(P, l, need_ctx)
        zero_missing(P)
        fw.barrier()
        if stage == 5:
            break
        phase_c(P, l)
    fw.barrier()
    P.dbg = {}
    for nm in getattr(c, "debug", ()):
        t = P.dr[nm]
        o = P.nc.dram_tensor("dbg_" + nm, list(t.shape), t.dtype, kind="ExternalOutput")
        P.dbg[nm] = o
        fw.dma(o.ap()[:, :], t.ap()[:, :], queue="sp")
    fw.barrier()
    return P


def _bf16(a):
    return np.asarray(a, dtype=np.float32).astype(ml_dtypes.bfloat16)


def dft_consts(cfg):
    T, L = cfg.T, cfg.L
    d = np.arange(128)
    ang = 2 * np.pi * np.outer(d, d) / 128.0
    cdsd = np.concatenate([np.cos(ang), np.sin(ang)], 1) / np.sqrt(128.0)
    t = np.arange(T, dtype=np.int64)
    m = (np.outer(t, t) % T).astype(np.float64)
    ct = np.cos(2 * np.pi * m / T) / np.sqrt(T)
    st = -np.sin(2 * np.pi * m / T) / np.sqrt(T)
    tl = np.arange(L, dtype=np.int64)
    ml_ = (np.outer(tl, tl) % L).astype(np.float64)
    ccs = np.concatenate([np.cos(2 * np.pi * ml_ / L), -np.sin(2 * np.pi * ml_ / L)], 1) / np.sqrt(L)
    return cdsd.astype(np.float32), _bf16(ct), _bf16(st), _bf16(ccs)


def dn_consts():
    i = np.arange(128)[:, None]
    j = np.arange(128)[None, :]
    same = (i // 64) == (j // 64)
    ident = (i == j)
    Mf = same & (i <= j)
    Mb = same & (i >= j)
    Mch = same
    ms_f = same & (i > j)
    ms_b = same & (i < j)
    miT_f = same & (j >= i)
    miT_b = same & (j <= i)
    ind2 = np.zeros((128, 128), bool)
    ind2[:64, 0] = True
    ind2[64:, 1] = True
    return np.concatenate([x.astype(np.float32) for x in (ident, Mf, Mb, Mch, ms_f, ms_b, miT_f, miT_b, ind2)], 1)


def na_bias_tables(cfg, rpb):
    NL = rpb.shape[0]
    ROWS, NT = cfg.ROWS, cfg.NT
    out = np.full((NL, 4, 35, 128, 128), -30000.0, np.float32)
    col = np.arange(64)
    cstart = np.clip(col - 8, 0, 64 - 16)
    inwin = (col[None, :] >= cstart[:, None]) & (col[None, :] < cstart[:, None] + 16)
    dc = np.clip(col[None, :] - col[:, None], -15, 15) + 15
    kr_n = min(8, ROWS)
    for cls in range(5):
        if cls < 2:
            jj = cls
        elif cls == 2:
            jj = 2
        else:
            jj = NT - 2 + (cls - 3)
        if cls == 2 and NT < 5:
            continue
        j = jj
        for dm in range(-3, 4):
            m = j + dm
            if m < 0 or m >= NT:
                continue
            for a in range(2):
                for b in range(2):
                    kr, qr = 2 * m + a, 2 * j + b
                    s0 = int(np.clip(qr - kr_n // 2, 0, ROWS - kr_n))
                    if not (s0 <= kr < s0 + kr_n):
                        continue
                    blk = rpb[:, :, kr - qr + 7, :][:, :, dc]
                    blk = np.where(inwin[None, None], blk, -30000.0)
                    out[:, :, cls * 7 + dm + 3, a * 64:(a + 1) * 64, b * 64:(b + 1) * 64] = blk.transpose(0, 1, 3, 2)
    return out


def make_in_maps(cfg, inp):
    c = cfg
    D, KT, NL, L, T = c.D, c.KT, c.NL, c.L, c.T
    f32 = np.float32
    g = lambda k: np.asarray(inp[k], dtype=f32)
    perm = win_perm()
    cdsd, ct, st, ccs = dft_consts(c)
    shared = {}
    shared["w_in_s"] = np.ascontiguousarray(g("w_in")[:, :, perm]).reshape(NL * D, -1)
    shared["w_gate_s"] = g("w_gate").reshape(-1, D)
    shared["w_branch_s"] = g("w_branch").reshape(-1, D)
    shared["w_out_s"] = g("w_out").reshape(-1, D)
    shared["w1_s"] = g("ffn_w1").reshape(-1, c.DFF)
    shared["w3_s"] = g("ffn_w3").reshape(-1, c.DFF)
    shared["w2_s"] = g("ffn_w2").reshape(-1, D)
    shared["w_ada_s"] = g("w_ada").reshape(NL * D, 6 * D)
    nft = 6 * D // 128
    shared["b_ada_s"] = np.ascontiguousarray(g("b_ada").reshape(NL, nft, 128).transpose(2, 0, 1).reshape(128, -1))
    shared["normw"] = np.ascontiguousarray(
        np.stack([g("norm1_w"), g("norm2_w")], 1).reshape(NL, 2, KT, 128).transpose(3, 0, 1, 2).reshape(128, -1))
    shared["b_gate"] = np.ascontiguousarray(g("b_gate").reshape(NL, 4, KT, 128).transpose(3, 0, 1, 2).reshape(128, -1))
    shared["hnorm"] = np.ascontiguousarray(
        np.stack([g("na_qnorm"), g("na_knorm"), g("dn_onorm")], 1).transpose(2, 0, 1).reshape(128, -1))
    shared["na_bias"] = na_bias_tables(c, g("na_rpb")).reshape(-1, 128)
    cv = g("dn_conv").reshape(NL, 5, 3, 4, 128)
    shared["dn_conv"] = np.ascontiguousarray(cv.transpose(4, 0, 2, 3, 1).reshape(128, -1))
    shared["dn_par"] = np.ascontiguousarray(
        np.concatenate([g("dn_a_log").reshape(NL, 8), g("dn_dt_bias").reshape(NL, 8)], 1).reshape(1, -1))
    shared["sg_wT"] = np.ascontiguousarray(g("sg_w").transpose(3, 0, 1, 2).reshape(128, -1))
    shared["sg_b"] = g("sg_b").reshape(1, -1)
    shared["sg_vn"] = g("sg_vnorm").reshape(1, -1)
    shared["cdsd"] = cdsd
    shared["ct_s"] = ct
    shared["st_s"] = st
    shared["ctx_cs"] = ccs
    shared["consts"] = dn_consts()
    maps = []
    for core in range(c.ncores):
        b = core // c.cores_per_batch
        m = dict(shared)
        m["xh"] = np.ascontiguousarray(g("x")[b].T)
        m["ctxT"] = np.ascontiguousarray(g("ctx")[b].T)
        sc2 = np.stack([g("c")[b], g("c_ctx")], 0)
        m["scT"] = np.ascontiguousarray(sc2.reshape(2, KT, 128).transpose(2, 1, 0).reshape(128, KT * 2))
        p = core % c.cores_per_batch
        m["offs"] = np.array([[p * (T // c.cores_per_batch), 0, 0, 0]], np.int32)
        maps.append(m)
    return maps


_CACHE = {}


def run(cfg, inp):
    key = (cfg.D, cfg.ROWS, cfg.NL, tuple(cfg.mixers), cfg.ncores)
    if key not in _CACHE:
        _CACHE[key] = build(cfg)
    P = _CACHE[key]
    maps = make_in_maps(cfg, inp)
    res = run_bass_kernel_spmd(P.nc, maps, core_ids=list(range(cfg.ncores)))
    B = cfg.ncores // cfg.cores_per_batch
    global LAST_RES
    LAST_RES = res.results
    out = np.zeros((B, cfg.T, cfg.D), np.float32)
    Th2 = cfg.T // cfg.cores_per_batch
    for core in range(cfg.ncores):
        b, p = core // cfg.cores_per_batch, core % cfg.cores_per_batch
        out[b, p * Th2:(p + 1) * Th2] = res.results[core]["out"].T
    return out


def kernel(**inputs):
    cfg = Cfg()
    cfg.mixers = ("sgu", "na", "ft", "dn")
    cfg.ncores = 8
    cfg.cores_per_batch = 2
    return run(cfg, inputs)
```
